# Optimizing a Trainium2 kernel written in Bass

```python
import jax
import jax.numpy as jnp
from jax import lax
import numpy as np

D_MODEL = 1024
BATCH = 32
SEQ = 2048
DEPTH = 2

CHUNK = 64
Q_BLOCK = 128
MEM_LEN = 256

MLA_HEADS = 8
QK_NOPE_DIM = 64
QK_ROPE_DIM = 32
V_HEAD_DIM = 64
Q_LORA_RANK = 256
KV_LORA_RANK = 256
MLA_WIDTH = MLA_HEADS * V_HEAD_DIM
CONV_WIDTH = D_MODEL - MLA_WIDTH
CONV_TAPS = 31
IN_COLS = Q_LORA_RANK + KV_LORA_RANK + QK_ROPE_DIM + 2 * CONV_WIDTH

XA_HEADS = 4
XA_HEAD_DIM = D_MODEL // XA_HEADS

FFN_HIDDEN = ((8 * D_MODEL + 3 * 256 - 1) // (3 * 256)) * 256

ALPHA = (2.0 * DEPTH) ** 0.25
BETA = (8.0 * DEPTH) ** -0.25

ROPE_BASE = 10000.0
LN_EPS = 1e-5
RMS_EPS = 1e-6

kernel_name = "hybrid_mla_conformer_deepnorm_encoder"


def layer_norm(x, g, b):
    xf = x.astype(jnp.float32)
    mu = jnp.mean(xf, axis=-1, keepdims=True)
    var = jnp.mean(jnp.square(xf - mu), axis=-1, keepdims=True)
    y = (xf - mu) * lax.rsqrt(var + LN_EPS)
    return (y * g.astype(jnp.float32) + b.astype(jnp.float32)).astype(x.dtype)


def rms_norm(x, g):
    xf = x.astype(jnp.float32)
    y = xf * lax.rsqrt(jnp.mean(jnp.square(xf), axis=-1, keepdims=True) + RMS_EPS)
    return (y * g.astype(jnp.float32)).astype(x.dtype)


def rope_cos_sin(positions, dim):
    inv_freq = ROPE_BASE ** (-jnp.arange(0, dim, 2, dtype=jnp.float32) / dim)
    ang = positions.astype(jnp.float32)[..., None] * inv_freq
    return jnp.cos(ang), jnp.sin(ang)


def apply_rope(x, cos, sin):
    half = x.shape[-1] // 2
    x1 = x[..., :half].astype(jnp.float32)
    x2 = x[..., half:].astype(jnp.float32)
    return jnp.concatenate([x1 * cos - x2 * sin, x1 * sin + x2 * cos], axis=-1).astype(x.dtype)


def mla_group(c_q, c_kv, k_r, cos, sin, q_norm_g, w_uq, kv_norm_g, w_ukv):
    B, S, _ = c_q.shape
    q = (rms_norm(c_q, q_norm_g) @ w_uq).reshape(B, S, MLA_HEADS, QK_NOPE_DIM + QK_ROPE_DIM)
    q_nope, q_rope = q[..., :QK_NOPE_DIM], q[..., QK_NOPE_DIM:]
    q_rope = apply_rope(q_rope, cos[:, :, None, :], sin[:, :, None, :])
    kv = (rms_norm(c_kv, kv_norm_g) @ w_ukv).reshape(B, S, MLA_HEADS, QK_NOPE_DIM + V_HEAD_DIM)
    k_nope, v = kv[..., :QK_NOPE_DIM], kv[..., QK_NOPE_DIM:]
    k_rope = apply_rope(k_r, cos, sin)
    scale = (QK_NOPE_DIM + QK_ROPE_DIM) ** -0.5
    outs = []
    for start in range(0, S, Q_BLOCK):
        end = start + Q_BLOCK
        s = (jnp.einsum('bqhd,bkhd->bhqk', q_nope[:, start:end], k_nope[:, :end])
             + jnp.einsum('bqhr,bkr->bhqk', q_rope[:, start:end], k_rope[:, :end]))
        q_chunk = jnp.arange(start, end) // CHUNK
        k_chunk = jnp.arange(end) // CHUNK
        mask = k_chunk[None, :] <= q_chunk[:, None]
        s = jnp.where(mask[None, None], s.astype(jnp.float32) * scale, -jnp.inf)
        p = jax.nn.softmax(s, axis=-1).astype(v.dtype)
        outs.append(jnp.einsum('bhqk,bkhd->bqhd', p, v[:, :end]))
    o = jnp.concatenate(outs, axis=1)
    return o.reshape(B, S, MLA_WIDTH)


def conformer_conv_group(u, dw_w, dw_b, norm_g, norm_b):
    a, g = jnp.split(u, 2, axis=-1)
    h = a * jax.nn.sigmoid(g)
    h = jnp.pad(h, ((0, 0), (CONV_TAPS - 1, 0), (0, 0)))
    h = lax.conv_general_dilated(
        h, dw_w[:, None, :], window_strides=(1,), padding='VALID',
        dimension_numbers=('NWC', 'WIO', 'NWC'), feature_group_count=CONV_WIDTH) + dw_b
    h = layer_norm(h, norm_g, norm_b)
    return jax.nn.silu(h)


def hybrid_mixer(x, cos, sin, w_in, b_in, q_norm_g, w_uq, kv_norm_g, w_ukv,
                 dw_w, dw_b, cn_g, cn_b, w_o):
    proj = x @ w_in + b_in
    o1 = Q_LORA_RANK
    o2 = o1 + KV_LORA_RANK
    o3 = o2 + QK_ROPE_DIM
    c_q, c_kv, k_r, u = proj[..., :o1], proj[..., o1:o2], proj[..., o2:o3], proj[..., o3:]
    y_att = mla_group(c_q, c_kv, k_r, cos, sin, q_norm_g, w_uq, kv_norm_g, w_ukv)
    y_conv = conformer_conv_group(u, dw_w, dw_b, cn_g, cn_b)
    return jnp.concatenate([y_att, y_conv], axis=-1) @ w_o


def memory_cross_attention(x, mem, w_q, w_kv, w_o):
    B, S, D = x.shape
    M = mem.shape[1]
    q = (x @ w_q).reshape(B, S, XA_HEADS, XA_HEAD_DIM)
    kv = (mem @ w_kv).reshape(B, M, 2, XA_HEADS, XA_HEAD_DIM)
    k, v = kv[:, :, 0], kv[:, :, 1]
    s = jnp.einsum('bshd,bmhd->bhsm', q, k).astype(jnp.float32) * (XA_HEAD_DIM ** -0.5)
    p = jax.nn.softmax(s, axis=-1).astype(v.dtype)
    o = jnp.einsum('bhsm,bmhd->bshd', p, v).reshape(B, S, D)
    return o @ w_o


def swiglu_ffn(x, w_in, w_down):
    gu = x @ w_in
    g, up = gu[..., :FFN_HIDDEN], gu[..., FFN_HIDDEN:]
    return (jax.nn.silu(g) * up) @ w_down


def setup_inputs(seed: int = 0) -> dict:
    key = jax.random.key(seed)
    ks = jax.random.split(key, 32)
    L, D = DEPTH, D_MODEL

    def normal(k, shape, fan_in, scale=1.0):
        return jax.random.normal(k, shape, jnp.float32) * (scale * fan_in ** -0.5)

    def gain(k, shape):
        return 1.0 + 0.02 * jax.random.normal(k, shape, jnp.float32)

    def small(k, shape):
        return 0.02 * jax.random.normal(k, shape, jnp.float32)

    x = jax.random.normal(ks[0], (BATCH, SEQ, D), jnp.float32)
    mem = jax.random.normal(ks[1], (BATCH, MEM_LEN, D), jnp.float32)
    offset = jax.random.randint(ks[2], (BATCH, 1), 0, 4096, dtype=jnp.int32)
    positions = offset + jnp.arange(SEQ, dtype=jnp.int32)[None, :]

    w_uk = normal(ks[7], (L, KV_LORA_RANK, MLA_HEADS, QK_NOPE_DIM), KV_LORA_RANK)
    w_uv = normal(ks[8], (L, KV_LORA_RANK, MLA_HEADS, V_HEAD_DIM), KV_LORA_RANK, BETA)
    mla_w_ukv = jnp.concatenate([w_uk, w_uv], axis=-1).reshape(
        L, KV_LORA_RANK, MLA_HEADS * (QK_NOPE_DIM + V_HEAD_DIM))
    xa_w_k = normal(ks[18], (L, D, D), D)
    xa_w_v = normal(ks[19], (L, D, D), D, BETA)
    xa_w_kv = jnp.concatenate([xa_w_k, xa_w_v], axis=-1)

    return {
        "x": x,
        "mem": mem,
        "positions": positions,
        "mix_w_in": normal(ks[3], (L, D, IN_COLS), D),
        "mix_b_in": small(ks[4], (L, IN_COLS)),
        "mla_q_norm": gain(ks[5], (L, Q_LORA_RANK)),
        "mla_w_uq": normal(ks[6], (L, Q_LORA_RANK, MLA_HEADS * (QK_NOPE_DIM + QK_ROPE_DIM)), Q_LORA_RANK),
        "mla_kv_norm": gain(ks[9], (L, KV_LORA_RANK)),
        "mla_w_ukv": mla_w_ukv,
        "conv_dw_w": normal(ks[10], (L, CONV_TAPS, CONV_WIDTH), CONV_TAPS),
        "conv_dw_b": small(ks[11], (L, CONV_WIDTH)),
        "conv_norm_g": gain(ks[12], (L, CONV_WIDTH)),
        "conv_norm_b": small(ks[13], (L, CONV_WIDTH)),
        "mix_w_o": normal(ks[14], (L, D, D), D, BETA),
        "ln1_g": gain(ks[15], (L, D)),
        "ln1_b": small(ks[16], (L, D)),
        "xa_w_q": normal(ks[17], (L, D, D), D),
        "xa_w_kv": xa_w_kv,
        "xa_w_o": normal(ks[20], (L, D, D), D, BETA),
        "ln2_g": gain(ks[21], (L, D)),
        "ln2_b": small(ks[22], (L, D)),
        "ffn_w_in": normal(ks[23], (L, D, 2 * FFN_HIDDEN), D),
        "ffn_w_down": normal(ks[24], (L, FFN_HIDDEN, D), FFN_HIDDEN, BETA),
        "ln3_g": gain(ks[25], (L, D)),
        "ln3_b": small(ks[26], (L, D)),
    }


def reference(x, mem, positions, mix_w_in, mix_b_in, mla_q_norm, mla_w_uq, mla_kv_norm,
              mla_w_ukv, conv_dw_w, conv_dw_b, conv_norm_g, conv_norm_b, mix_w_o,
              ln1_g, ln1_b, xa_w_q, xa_w_kv, xa_w_o, ln2_g, ln2_b,
              ffn_w_in, ffn_w_down, ln3_g, ln3_b):
    cos, sin = rope_cos_sin(positions, QK_ROPE_DIM)
    for l in range(DEPTH):
        y = hybrid_mixer(x, cos, sin, mix_w_in[l], mix_b_in[l], mla_q_norm[l], mla_w_uq[l],
                         mla_kv_norm[l], mla_w_ukv[l], conv_dw_w[l], conv_dw_b[l],
                         conv_norm_g[l], conv_norm_b[l], mix_w_o[l])
        x = layer_norm(ALPHA * x + y, ln1_g[l], ln1_b[l])
        y = memory_cross_attention(x, mem, xa_w_q[l], xa_w_kv[l], xa_w_o[l])
        x = layer_norm(ALPHA * x + y, ln2_g[l], ln2_b[l])
        y = swiglu_ffn(x, ffn_w_in[l], ffn_w_down[l])
        x = layer_norm(ALPHA * x + y, ln3_g[l], ln3_b[l])
    return x
```

```python
import contextlib
import math
import numpy as np
import concourse.bass as bass
import concourse.mybir as mybir
from concourse.bass_utils import run_bass_kernel_spmd

F32 = mybir.dt.float32
BF16 = mybir.dt.bfloat16
I32 = mybir.dt.int32
AF = mybir.ActivationFunctionType
ALU = mybir.AluOpType

D = 1024
SEQ = 2048
MEM = 256
DEPTH = 2
BT = 512
NBLK = SEQ // BT
NH = 8
IN_COLS = 1568
FFN_H = 2816
NHC = FFN_H // 128
ALPHA = (2.0 * DEPTH) ** 0.25
LN_EPS = 1e-5
RMS_EPS = 1e-6
NCOL = 208
N_CORES = 8
RL_ENG = "none"
DG_ENG = "act"
XB23 = ("act", "dve")
SEM_CAP = 16000
TWO_PI = 2.0 * math.pi
CW1 = 6.28125
CW2 = TWO_PI - CW1
PI_SAFE = 3.14159

WNAMES = ["mix_w_in", "mla_w_uq", "mla_w_ukv", "mix_w_o", "xa_w_kv", "xa_w_q", "xa_w_o",
          "ffn_w_in", "ffn_w_down"]
WSHAPES = {"mix_w_in": (1024, 1568), "mla_w_uq": (256, 768), "mla_w_ukv": (256, 1024),
           "mix_w_o": (1024, 1024), "xa_w_kv": (1024, 2048), "xa_w_q": (1024, 1024),
           "xa_w_o": (1024, 1024), "ffn_w_in": (1024, 5632), "ffn_w_down": (2816, 1024)}


class Prog:
    ENGS = ("pe", "act", "dve", "pool", "sp")

    def __init__(self, nc):
        self.nc = nc
        self.ops = []
        self.last_w = {}
        self.readers = {}
        self.dma_slots = {}
        self.eng_ops = {e: [] for e in self.ENGS}
        self.bar = {e: set() for e in self.ENGS}

    def op(self, eng, fn, reads=(), writes=(), dma_slot=None, grouped=False):
        i = len(self.ops)
        deps = set(self.bar[eng])
        self.bar[eng] = set()
        for t in reads:
            w = self.last_w.get(t)
            if w is not None:
                deps.add(w)
        for t in writes:
            w = self.last_w.get(t)
            if w is not None:
                deps.add(w)
            deps.update(self.readers.get(t, ()))
        rec = dict(id=i, eng=eng, fn=fn, deps=deps, dma=dma_slot, ms=None, dmaval=None)
        if dma_slot is not None:
            st = self.dma_slots.setdefault(dma_slot, dict(count=0, last=None, grouped=grouped))
            if st["last"] is not None and not grouped:
                deps.add(st["last"])
            st["count"] += 1
            st["last"] = i
            rec["dmaval"] = st["count"]
        deps.discard(i)
        self.ops.append(rec)
        self.eng_ops[eng].append(rec)
        for t in reads:
            self.readers.setdefault(t, []).append(i)
        for t in writes:
            self.last_w[t] = i
            self.readers[t] = []
        return i

    def barrier(self):
        last = set()
        for e in self.ENGS:
            if self.eng_ops[e]:
                last.add(self.eng_ops[e][-1]["id"])
        for st in self.dma_slots.values():
            if st["last"] is not None:
                last.add(st["last"])
        for e in self.ENGS:
            self.bar[e] |= last

    def emit(self, stack):
        nc = self.nc
        ops = self.ops
        needed = set()
        for r in ops:
            for d in r["deps"]:
                p = ops[d]
                if p["dma"] is not None:
                    continue
                if p["eng"] == "pe" and r["eng"] == "pe" and r["dma"] is None:
                    continue
                needed.add(d)
        cnt = {e: 0 for e in self.ENGS}
        for r in ops:
            if r["id"] in needed:
                cnt[r["eng"]] += 1
                r["ms"] = cnt[r["eng"]]
        esems = {}
        for e in self.ENGS:
            n = max(1, (cnt[e] + SEM_CAP - 1) // SEM_CAP)
            esems[e] = [stack.enter_context(nc.semaphore(f"s_{e}{k}")) for k in range(n)]
        per = SEM_CAP // 16
        dsems = {}
        for s, st in self.dma_slots.items():
            n = max(1, (st["count"] + per - 1) // per)
            j = len(dsems)
            dsems[s] = [stack.enter_context(nc.semaphore(f"d{j}_{k}")) for k in range(n)]
        self.n_sems = sum(len(v) for v in esems.values()) + sum(len(v) for v in dsems.values())

        def run_engine(ename):
            def body(eng):
                waited = {}
                for r in self.eng_ops[ename]:
                    reqs = {}
                    for d in r["deps"]:
                        p = ops[d]
                        if p["dma"] is not None:
                            dv = self.dma_slots[p["dma"]]["count"] if self.dma_slots[p["dma"]]["grouped"] else p["dmaval"]
                            k = (dv - 1) // per
                            v = ((dv - 1) % per + 1) * 16
                            key = ("d", p["dma"], k)
                            sem = dsems[p["dma"]][k]
                        else:
                            if p["eng"] == "pe" and ename == "pe" and r["dma"] is None:
                                continue
                            m = p["ms"]
                            k = (m - 1) // SEM_CAP
                            v = (m - 1) % SEM_CAP + 1
                            key = ("e", p["eng"], k)
                            sem = esems[p["eng"]][k]
                        if v > reqs.get(key, (0, None))[0]:
                            reqs[key] = (v, sem)
                    for key, (v, sem) in reqs.items():
                        if waited.get(key, 0) >= v:
                            continue
                        waited[key] = v
                        eng.wait_ge(sem, v)
                    ins = r["fn"](eng)
                    if r["dma"] is not None:
                        k = (r["dmaval"] - 1) // per
                        ins.then_inc(dsems[r["dma"]][k], 16)
                    elif r["ms"] is not None:
                        k = (r["ms"] - 1) // SEM_CAP
                        ins.then_inc(esems[ename][k], 1)
            return body

        with nc.Block() as block:
            block.tensor(run_engine("pe"))
            block.scalar(run_engine("act"))
            block.vector(run_engine("dve"))
            block.gpsimd(run_engine("pool"))
            block.sync(run_engine("sp"))


class Arena:
    def __init__(self, t, total):
        self.t, self.total, self.off, self.base = t, total, 0, 0
        self.peak = 0

    def _take(self, n):
        a = self.off
        self.off += n
        self.peak = max(self.peak, self.off)
        assert self.off <= self.total, f"SBUF arena overflow {self.off} > {self.total}"
        return self.t[:, a:a + n]

    def f32(self, n):
        return self._take(n)

    def bf(self, n):
        assert n % 2 == 0
        return self._take(n // 2).bitcast(BF16)

    def i32(self, n):
        return self._take(n).bitcast(I32)

    def mark(self):
        self.base = self.off

    def reset(self):
        self.off = self.base


def build(nseq=4, depth=DEPTH, stop_after=None):
    nc = bass.Bass("TRN2", target_bir_lowering=False)
    x_d = nc.dram_tensor("x", [nseq, SEQ, D], F32, kind="ExternalInput").ap()
    mem_d = nc.dram_tensor("mem", [nseq, MEM, D], F32, kind="ExternalInput").ap()
    pos_d = nc.dram_tensor("positions", [nseq, SEQ], I32, kind="ExternalInput").ap()
    pcols_d = nc.dram_tensor("pcols", [128, DEPTH * NCOL], F32, kind="ExternalInput").ap()
    cvec_d = nc.dram_tensor("cvec", [128, 4], F32, kind="ExternalInput").ap()
    ident_d = nc.dram_tensor("ident", [128, 128], F32, kind="ExternalInput").ap()
    w32, wbf, wrl = {}, {}, {}
    for n in WNAMES:
        r, c = WSHAPES[n]
        w32[n] = nc.dram_tensor(n, [DEPTH, r, c], F32, kind="ExternalInput").ap()
        wbf[n] = nc.dram_tensor(n + "_bf", [DEPTH, r, c], BF16, kind="Internal").ap()
        if n == "ffn_w_in":
            wrl[n] = nc.dram_tensor(n + "_rl", [DEPTH, NHC // 2, 128, 8 * 512], BF16, kind="Internal").ap()
        elif n == "ffn_w_down":
            wrl[n] = nc.dram_tensor(n + "_rl", [DEPTH, 8, 128, NHC * 128], BF16, kind="Internal").ap()
    out_d = nc.dram_tensor("out", [nseq, SEQ, D], F32, kind="ExternalOutput").ap()

    with contextlib.ExitStack() as st:
        TOTAL = 53100
        arena_t = st.enter_context(nc.sbuf_tensor("arena", [128, TOTAL], F32))
        ps = [st.enter_context(nc.psum_tensor(f"ps{i}", [128, 512], F32)) for i in range(8)]
        PST = [f"ps{i}" for i in range(8)]
        A = Arena(arena_t, TOTAL)
        P = Prog(nc)

        def MM(out, lhsT, rhs, start, stop, reads, wtok):
            P.op("pe", lambda e: e.matmul(out, lhsT=lhsT, rhs=rhs, start=start, stop=stop), reads, [wtok])

        def TR(out, in_, ident, reads, wtok):
            P.op("pe", lambda e: e.transpose(out=out, in_=in_, identity=ident), reads, [wtok])

        def ACT(out, in_, func, reads, writes, scale=None, bias=None):
            kw = {}
            if scale is not None:
                kw["scale"] = scale
            if bias is not None:
                kw["bias"] = bias
            P.op("act", lambda e: e.activation(out=out, in_=in_, func=func, **kw), reads, writes)

        def TT(eng, out, in0, in1, op, reads, writes):
            P.op(eng, lambda e: e.tensor_tensor(out=out, in0=in0, in1=in1, op=op), reads, writes)

        def TS(eng, out, in0, s1, s2, op0, op1, reads, writes):
            if eng == "act":
                assert op1 is None and op0 == ALU.mult
                P.op("act", lambda e: e.activation(out=out, in_=in0, func=AF.Copy, scale=s1), reads, writes)
            elif op1 is None and eng == "pool" and op0 == ALU.mult:
                P.op(eng, lambda e: e.tensor_scalar(out=out, in0=in0, scalar1=s1, scalar2=0.0, op0=ALU.mult, op1=ALU.add), reads, writes)
            elif op1 is None:
                P.op(eng, lambda e: e.tensor_scalar(out=out, in0=in0, scalar1=s1, scalar2=None, op0=op0), reads, writes)
            else:
                P.op(eng, lambda e: e.tensor_scalar(out=out, in0=in0, scalar1=s1, scalar2=s2, op0=op0, op1=op1), reads, writes)

        def STT(out, in0, scalar, in1, op0, op1, reads, writes):
            P.op("dve", lambda e: e.scalar_tensor_tensor(out=out, in0=in0, scalar=scalar, in1=in1, op0=op0, op1=op1), reads, writes)

        def CP(eng, out, in_, reads, writes):
            if eng == "act":
                P.op("act", lambda e: e.activation(out=out, in_=in_, func=AF.Copy), reads, writes)
            else:
                P.op(eng, lambda e: e.tensor_copy(out=out, in_=in_), reads, writes)

        def MS(eng, ap, val, reads, writes):
            P.op(eng, lambda e: e.memset(ap, val), reads, writes)

        def DMA(eng, out, in_, reads, writes, slot, nonc=False, grouped=False):
            if grouped:
                P.op(eng, lambda e: e.dma_start(out=out, in_=in_), reads, writes, dma_slot=slot, grouped=True)
            elif nonc:
                def f(e):
                    with nc.allow_non_contiguous_dma(reason="small strided layout load"):
                        return e.dma_start(out=out, in_=in_)
                P.op(eng, f, reads, writes, dma_slot=slot)
            else:
                P.op(eng, lambda e: e.dma_start(out=out, in_=in_), reads, writes, dma_slot=slot)

        resid = A.f32(8 * SEQ).rearrange("p (c t) -> p c t", c=8)
        ident_f = A.f32(128)
        ones_s = A.bf(128)
        ones_d = A.bf(128)
        ones_c = A.bf(128)
        ones_r = A.bf(128)
        identb = A.bf(128)
        cvec = A.f32(4)
        pcols = A.f32(DEPTH * NCOL)
        memT = A.bf(8 * MEM).rearrange("p (c t) -> p c t", c=8)
        A.mark()

        def RT(b, c):
            return ("resid", b, c)

        def RTA(b):
            return [("resid", b, c) for c in range(8)]

        DMA("sp", ident_f, ident_d, [], ["ident"], "c0")
        DMA("sp", cvec, cvec_d, [], ["cvec"], "c1")
        DMA("sp", pcols, pcols_d, [], ["pcols"], "c2")
        for l in range(depth):
            for n in WNAMES:
                r, c = WSHAPES[n]
                k = c if c <= 1568 else (c // 2 if c // 2 <= 1568 else c // 4)
                src = w32[n][l].rearrange("r (a k) -> (r a) k", k=k)
                dst = wbf[n][l].rearrange("r (a k) -> (r a) k", k=k)
                DMA("pool", dst, src, [], [("wbf", n, l)], f"cast_{n}_{l}")
        MS("pool", ones_s, 1.0, [], ["ones_s"])
        MS("pool", ones_d, 1.0 / 1024, [], ["ones_d"])
        MS("pool", ones_c, 1.0 / 512, [], ["ones_c"])
        MS("pool", ones_r, 1.0 / 256, [], ["ones_r"])
        CP("dve", identb, ident_f, ["ident"], ["identb"])
        for l in range(depth):
            o = l * NCOL
            TT("dve", pcols[:, o + 202:o + 206], pcols[:, o + 0:o + 4], pcols[:, o + 14:o + 18], ALU.mult,
               ["pcols"], ["pcols"])

        def pc(l, j):
            return pcols[:, l * NCOL + j:l * NCOL + j + 1]

        uid = [0]
        outtoks = []

        def newsweep():
            P.barrier()
            A.reset()
            uid[0] += 1
            u = uid[0]
            return lambda *a: (u,) + a

        def layer_norm(T, zsrc, ztoks, C, ones_m, eps, gcol, bcol, dst_fn, dtoks, tmp, psA, psB, silu=False, phase=0,
                       pool_chain=False):
            zb, zsq, t1, t2, t3 = tmp["zb"], tmp["zsq"], tmp["t1"], tmp["t2"], tmp["t3"]
            if phase in (0, 1):
                for c in range(C):
                    CP("dve", zb[:, c, :], zsrc(c), ztoks(c), [T("zb", c)])
                    ACT(zsq[:, c, :], zsrc(c), AF.Square, ztoks(c), [T("zsq", c)])
            if phase == 1:
                return
            for c in range(C):
                MM(ps[psA][:, :], ones_m, zb[:, c, :], c == 0, c == C - 1, [T("zb", c), "ones"], PST[psA])
            for c in range(C):
                MM(ps[psB][:, :], ones_m, zsq[:, c, :], c == 0, c == C - 1, [T("zsq", c), "ones"], PST[psB])
            ACT(t1, ps[psA][:, :], AF.Square, [PST[psA]], [T("t1")])
            TT("dve", t2, ps[psB][:, :], t1, ALU.subtract, [PST[psB], T("t1")], [T("t2")])
            ACT(t2, t2, AF.Ln, [T("t2"), T("eps")], [T("t2")], bias=tmp["eps"])
            ACT(t1, t2, AF.Exp, [T("t2"), T("t1")], [T("t1")], scale=-0.5)
            STT(t3, ps[psA][:, :], -1.0, t1, ALU.mult, ALU.mult, [PST[psA], T("t1")], [T("t3")])
            for c in range(C):
                z = zsrc(c)
                zt = tmp["zt"][:, c % 2, :]
                if pool_chain:
                    TT("pool", zt, z, t1, ALU.mult, ztoks(c) + [T("t1")], [T("zt", c % 2)])
                    TT("pool", zt, zt, t3, ALU.add, [T("zt", c % 2), T("t3")], [T("zt", c % 2)])
                    TS("pool", dst_fn(c), zt, gcol(c), bcol(c), ALU.mult, ALU.add, [T("zt", c % 2), "pcols"], dtoks(c))
                    continue
                TT("pool", zt, z, t1, ALU.mult, ztoks(c) + [T("t1")], [T("zt", c % 2)])
                TT("dve", zt, zt, t3, ALU.add, [T("zt", c % 2), T("t3")], [T("zt", c % 2)])
                ACT(dst_fn(c), zt, AF.Silu if silu else AF.Identity, [T("zt", c % 2), "pcols"], dtoks(c),
                    scale=gcol(c), bias=bcol(c))

        def ln_tmp(C):
            d = dict(zb=A.bf(C * 512).rearrange("p (c t) -> p c t", c=C),
                     zsq=A.bf(C * 512).rearrange("p (c t) -> p c t", c=C),
                     t1=A.f32(512), t2=A.f32(512), t3=A.f32(512),
                     zt=A.f32(1024).rearrange("p (c t) -> p c t", c=2), eps=A.f32(1))
            return d

        for s in range(nseq):
            T = newsweep()
            xts = [A.f32(1024), A.f32(1024)]
            n_t = 0
            for tt in range(SEQ // 128 + MEM // 128):
                is_mem = tt >= SEQ // 128
                xt = xts[tt % 2]
                src = mem_d[s, (tt - 16) * 128:(tt - 15) * 128, :] if is_mem else x_d[s, tt * 128:(tt + 1) * 128, :]
                DMA("sp", xt, src, [], [T("xt", tt % 2)], f"xt{tt % 2}")
                for half in range(2):
                    bk = n_t % 8
                    n_t += 1
                    for j in range(4):
                        c = half * 4 + j
                        TR(ps[bk][:, j * 128:(j + 1) * 128], xt[:, c * 128:(c + 1) * 128], ident_f,
                           [T("xt", tt % 2), "ident"], PST[bk])
                    src_v = ps[bk][:, :].rearrange("p (j t) -> p j t", j=4)
                    if is_mem:
                        mt = tt - 16
                        CP("act", memT[:, half * 4:half * 4 + 4, mt * 128:(mt + 1) * 128], src_v, [PST[bk]], ["memT"])
                    else:
                        CP("dve", resid[:, half * 4:half * 4 + 4, tt * 128:(tt + 1) * 128], src_v, [PST[bk]],
                           [RT(tt // 4, half * 4 + j) for j in range(4)])

            for l in range(depth):
                T = newsweep()
                XBENG = ("dve", "dve")
                w1 = A.bf(8 * 544).rearrange("p (k n) -> p k n", k=8)
                wks = A.bf(8 * 96).rearrange("p (k n) -> p k n", k=8)
                wqa = A.bf(2 * 768).rearrange("p (k n) -> p k n", k=2)
                wqb = A.bf(2 * 768).rearrange("p (k n) -> p k n", k=2)
                wkv = A.bf(2 * 1024).rearrange("p (k n) -> p k n", k=2)
                KT = A.bf(8 * SEQ).rearrange("p (h t) -> p h t", h=8)
                VV = A.bf(16 * 512).rearrange("p (k n) -> p k n", k=16)
                yatt = A.bf(4 * SEQ).rearrange("p (c t) -> p c t", c=4)
                xb = A.bf(8 * 512).rearrange("p (c t) -> p c t", c=8)
                cg = A.bf(4 * 512).rearrange("p (c t) -> p c t", c=4)
                csq = A.bf(4 * 512).rearrange("p (c t) -> p c t", c=4)
                rq = A.f32(512)
                srt = A.f32(512)
                rkv = A.f32(512)
                cosT = A.f32(512)
                sinT = A.f32(512)
                posi = A.i32(512)
                ang = A.f32(512)
                kf = A.f32(512)
                qT = A.bf(8 * 512).rearrange("p (h t) -> p h t", h=8)
                tA, tB = ang, kf
                krb = posi.bitcast(BF16)[:, 0:512]
                exps = [csq[:, i, :] for i in range(3)]
                rcp = [A.f32(512) for _ in range(2)]
                rtok = A.f32(8)
                epsr = A.f32(1)
                MS("pool", epsr, RMS_EPS, [], [T("epsr")])
                win = wbf["mix_w_in"][l].rearrange("(k p) n -> p k n", p=128)
                wt = ("wbf", "mix_w_in", l)
                DMA("sp", w1, win[:, :, 0:544], [wt], [T("w1")], "wA")
                DMA("sp", wks[:, :, 0:64], win[:, :, 448:512], [wt], [T("wks")], "wB")
                DMA("sp", wks[:, :, 64:80], win[:, :, 528:544], [wt], [T("wks")], "wB", nonc=True)
                DMA("sp", wks[:, :, 80:96], win[:, :, 512:528], [wt], [T("wks")], "wB", nonc=True)
                wuq = wbf["mla_w_uq"][l].rearrange("(k p) n -> p k n", p=128)
                wt = ("wbf", "mla_w_uq", l)
                DMA("sp", wqa, wuq, [wt], [T("wqa")], "wC")
                DMA("sp", wqb, wuq, [wt], [T("wqb")], "wD")
                wqb4 = wqb.rearrange("p k (h d) -> p k h d", h=8)
                wuq4 = wuq.rearrange("p k (h d) -> p k h d", h=8)
                for k in range(2):
                    DMA("sp", wqb4[:, k, :, 64:80], wuq4[:, k, :, 80:96], [wt], [T("wqb")], "wD", nonc=True)
                    DMA("sp", wqb4[:, k, :, 80:96], wuq4[:, k, :, 64:80], [wt], [T("wqb")], "wD", nonc=True)
                DMA("sp", wkv, wbf["mla_w_ukv"][l].rearrange("(k p) n -> p k n", p=128), [("wbf", "mla_w_ukv", l)],
                    [T("wkv")], "wE")
                wkv4 = wkv.rearrange("p k (h d) -> p k h d", h=8)
                sc = 96.0 ** -0.5
                for b in range(NBLK):
                    t0 = b * BT
                    for c in range(8):
                        CP(XBENG[c % 2], xb[:, c, :], resid[:, c, t0:t0 + BT], [RT(b, c)], [T("xb", c)])
                    xbt = [T("xb", c) for c in range(8)]
                    R = slice(64, 96)
                    DMA("sp", posi[R, :], pos_d[s:s + 1, t0:t0 + BT].partition_broadcast(32), [], [T("posi")], "pos")
                    CP("dve", ang[R, :], posi[R, :], [T("posi")], [T("ang")])
                    TS("dve", ang[R, :], ang[R, :], cvec[R, 0:1], None, ALU.mult, None, [T("ang"), "cvec"], [T("ang")])
                    TS("dve", posi[R, :], ang[R, :], 1.0 / TWO_PI, None, ALU.mult, None, [T("ang")], [T("posi")])
                    CP("dve", kf[R, :], posi[R, :], [T("posi")], [T("kf")])
                    STT(ang[R, :], kf[R, :], -CW1, ang[R, :], ALU.mult, ALU.add, [T("kf"), T("ang")], [T("ang")])
                    STT(ang[R, :], kf[R, :], -CW2, ang[R, :], ALU.mult, ALU.add, [T("kf"), T("ang")], [T("ang")])
                    TS("dve", kf[R, :], ang[R, :], math.pi / 2, None, ALU.add, None, [T("ang")], [T("kf")])
                    TS("dve", cosT[R, :], kf[R, :], math.pi, -TWO_PI, ALU.is_gt, ALU.mult, [T("kf")], [T("cosT")])
                    TT("dve", kf[R, :], kf[R, :], cosT[R, :], ALU.add, [T("kf"), T("cosT")], [T("kf")])
                    TS("dve", kf[R, :], kf[R, :], PI_SAFE, -PI_SAFE, ALU.min, ALU.max, [T("kf")], [T("kf")])
                    TS("dve", ang[R, :], ang[R, :], PI_SAFE, -PI_SAFE, ALU.min, ALU.max, [T("ang")], [T("ang")])
                    ACT(cosT[R, :], kf[R, :], AF.Sin, [T("kf")], [T("cosT")])
                    ACT(sinT[R, :], ang[R, :], AF.Sin, [T("ang")], [T("sinT")])
                    TS("dve", sinT[R, :], sinT[R, :], cvec[R, 1:2], None, ALU.mult, None, [T("sinT"), "cvec"], [T("sinT")])
                    for j in range(4):
                        bk = j
                        for k in range(8):
                            MM(ps[bk][:, :], w1[:, k, j * 128:(j + 1) * 128], xb[:, k, :], k == 0, k == 7,
                               [T("w1"), xbt[k]], PST[bk])
                        ACT(csq[:, j, :], ps[bk][:, :], AF.Square, [PST[bk], "pcols"], [T("csq", j)], bias=pc(l, j))
                        ACT(cg[:, j, :], ps[bk][:, :], AF.Identity, [PST[bk], "pcols"], [T("cg", j)],
                            scale=pc(l, 14 + j), bias=pc(l, 202 + j))
                    for k in range(8):
                        MM(ps[4][0:96, :], w1[:, k, 448:544], xb[:, k, :], k == 0, k == 7, [T("w1"), xbt[k]], PST[4])
                    for k in range(8):
                        MM(ps[5][0:96, :], wks[:, k, :], xb[:, k, :], k == 0, k == 7, [T("wks"), xbt[k]], PST[5])
                    ACT(tA[R, :], ps[4][R, :], AF.Identity, [PST[4], "pcols"], [T("ang")], bias=pcols[R, l * NCOL + 4:l * NCOL + 5])
                    ACT(tB[R, :], ps[5][R, :], AF.Identity, [PST[5], "pcols"], [T("kf")], bias=pcols[R, l * NCOL + 5:l * NCOL + 6])
                    TT("dve", tA[R, :], tA[R, :], cosT[R, :], ALU.mult, [T("ang"), T("cosT")], [T("ang")])
                    TT("dve", tB[R, :], tB[R, :], sinT[R, :], ALU.mult, [T("kf"), T("sinT")], [T("kf")])
                    TT("dve", krb[R, :], tA[R, :], tB[R, :], ALU.add, [T("ang"), T("kf")], [T("posi")])
                    for h in range(NH):
                        CP("dve" if h % 2 == 0 else "act", KT[R, h, t0:t0 + BT], krb[R, :], [T("posi")], [T("KT", h, b)])
                    for (j0, bk, dst) in ((0, 6, rq), (2, 7, rkv)):
                        for j in range(2):
                            MM(ps[bk][:, :], ones_r, csq[:, j0 + j, :], j == 0, j == 1, [T("csq", j0 + j), "ones_r"], PST[bk])
                        ACT(dst, ps[bk][:, :], AF.Ln, [PST[bk], T("epsr")], [T("r", j0)], bias=epsr)
                        ACT(dst, dst, AF.Exp, [T("r", j0)], [T("r", j0)], scale=-0.5)
                    for tt in range(4):
                        for j in range(2):
                            MM(ps[6][:, tt:tt + 1], csq[:, 2 + j, tt * 128:(tt + 1) * 128], ones_r[:, 0:1], j == 0, j == 1,
                               [T("csq", 2 + j), "ones_r", T("r", 0)], PST[6])
                    ACT(rtok[:, 0:4], ps[6][:, 0:4], AF.Ln, [PST[6], T("epsr")], [T("rtok")], bias=epsr)
                    ACT(rtok[:, 0:4], rtok[:, 0:4], AF.Exp, [T("rtok")], [T("rtok")], scale=-0.5)
                    TT("dve", srt[R, :], rq[R, :], sinT[R, :], ALU.mult, [T("r", 0), T("sinT")], [T("srt")])
                    TT("dve", rq[R, :], rq[R, :], cosT[R, :], ALU.mult, [T("r", 0), T("cosT"), T("srt")], [T("r", 0)])
                    for h in range(NH):
                        ba, bb = 2 * (h % 2), 2 * (h % 2) + 1
                        for k in range(2):
                            MM(ps[ba][0:96, :], wqa[:, k, h * 96:(h + 1) * 96], cg[:, k, :], k == 0, k == 1,
                               [T("wqa"), T("cg", k)], PST[ba])
                        for k in range(2):
                            MM(ps[bb][0:96, :], wqb[:, k, h * 96:(h + 1) * 96], cg[:, k, :], k == 0, k == 1,
                               [T("wqb"), T("cg", k)], PST[bb])
                        tq, tqt = (tA, T("ang")) if h % 2 == 0 else (tB, T("kf"))
                        TT("dve", qT[0:96, h, :], ps[ba][0:96, :], rq[0:96, :], ALU.mult, [PST[ba], T("r", 0)], [T("q", h)])
                        TT("dve", tq[R, :], ps[bb][R, :], srt[R, :], ALU.mult, [PST[bb], T("srt")], [tqt])
                        TT("dve", qT[R, h, :], qT[R, h, :], tq[R, :], ALU.add, [T("q", h), tqt], [T("q", h)])
                    for h in range(NH):
                        bk = 4 + (h % 2)
                        for k in range(2):
                            MM(ps[bk][0:64, :], wkv4[:, k, h, 0:64], cg[:, 2 + k, :], k == 0, k == 1,
                               [T("wkv"), T("cg", 2 + k)], PST[bk])
                        TT("dve", KT[0:64, h, t0:t0 + BT], ps[bk][0:64, :], rkv[0:64, :], ALU.mult, [PST[bk], T("r", 2)],
                           [T("KT", h, b)])
                    for tt in range(4):
                        bk = 6 + (tt % 2)
                        for k in range(2):
                            MM(ps[bk][:, :].rearrange("p (h d) -> p h d", h=8), cg[:, 2 + k, tt * 128:(tt + 1) * 128],
                               wkv4[:, k, :, 64:128], k == 0, k == 1, [T("wkv"), T("cg", 2 + k), T("rtok")], PST[bk])
                        ACT(VV[:, 4 * b + tt, :], ps[bk][:, :], AF.Identity, [PST[bk], T("rtok")], [T("V", 4 * b + tt)],
                            scale=rtok[:, tt:tt + 1])
                    pairs = []
                    for h in range(NH):
                        for kt in range(4 * b + 4):
                            pairs.append((h, kt))
                    nkt = 4 * b + 4

                    def score(i):
                        h, kt = pairs[i]
                        c0 = max(0, kt - 4 * b) * 128
                        bk = i % 3
                        MM(ps[bk][:, c0:BT], KT[0:96, h, kt * 128:(kt + 1) * 128], qT[0:96, h, c0:BT], True, True,
                           [T("KT", h, kt // 4), T("q", h)], PST[bk])
                        e = exps[i % 3]
                        ACT(e[:, c0:BT], ps[bk][:, c0:BT], AF.Exp, [PST[bk]], [T("csq", i % 3)], scale=sc)
                        if kt >= 4 * b:
                            MS("pool", e[64:128, c0:c0 + 64], 0.0, [], [T("csq", i % 3)])

                    def pv(i):
                        h, kt = pairs[i]
                        c0 = max(0, kt - 4 * b) * 128
                        e = exps[i % 3]
                        bo, bs = 3 + (h % 2), 5 + (h % 2)
                        if h % 2 == 0:
                            MM(ps[bo][0:64, c0:BT], VV[:, kt, h * 64:(h + 1) * 64], e[:, c0:BT], kt == 0, kt == nkt - 1,
                               [T("V", kt), T("csq", i % 3)], PST[bo])
                        else:
                            MM(ps[bo][:, c0:BT], VV[:, kt, (h - 1) * 64:(h + 1) * 64], e[:, c0:BT], kt == 0, kt == nkt - 1,
                               [T("V", kt), T("csq", i % 3)], PST[bo])
                        MM(ps[bs][:, c0:BT], ones_s, e[:, c0:BT], kt == 0, kt == nkt - 1, ["ones_s", T("csq", i % 3)], PST[bs])
                        if kt == nkt - 1:
                            rr = slice(0, 64) if h % 2 == 0 else slice(64, 128)
                            rc = rcp[h % 2]
                            ACT(rc[rr, :], ps[bs][rr, :], AF.Ln, [PST[bs]], [T("rc", h % 2)])
                            ACT(rc[rr, :], rc[rr, :], AF.Exp, [T("rc", h % 2)], [T("rc", h % 2)], scale=-1.0)
                            TT("dve", yatt[rr, h // 2, t0:t0 + BT], ps[bo][rr, :], rc[rr, :], ALU.mult,
                               [PST[bo], T("rc", h % 2)], [("yatt", b, h // 2)])

                    LA = 2
                    for i in range(len(pairs) + LA):
                        if i < len(pairs):
                            score(i)
                        if i >= LA:
                            pv(i - LA)
                if stop_after == (l, 1):
                    break

                T = newsweep()
                XBENG = XB23
                A.off = A.base
                _skip = A._take(8 * 544 // 2 + 8 * 96 // 2 + 768 + 768 + 1024 + 8 * SEQ // 2 + 16 * 512 // 2)
                yatt = A.bf(4 * SEQ).rearrange("p (c t) -> p c t", c=4)
                yatt_end = A.off
                A.off = A.base
                wu = A.bf(8 * 1024).rearrange("p (k n) -> p k n", k=8)
                wo = A.bf(8 * 1024).rearrange("p (k n) -> p k n", k=8)
                dg = [A.bf(31 * 128).rearrange("p (t n) -> p t n", t=31) for _ in range(4)]
                sg = [A.f32(512)] * 2
                assert A.off <= yatt_end - 4 * SEQ // 2, "sweep2 prefix overlaps yatt"
                A.off = yatt_end
                xb = A.bf(8 * 512).rearrange("p (c t) -> p c t", c=8)
                hh = A.bf(4 * 544).rearrange("p (c t) -> p c t", c=4)
                cv = A.f32(4 * 512).rearrange("p (c t) -> p c t", c=4)
                ycv = A.bf(4 * 512).rearrange("p (c t) -> p c t", c=4)
                lt = ln_tmp(8)
                MS("pool", lt["eps"], LN_EPS, [], [T("eps")])
                win = wbf["mix_w_in"][l].rearrange("(k p) n -> p k n", p=128)
                DMA("sp", wu, win[:, :, 544:1568], [("wbf", "mix_w_in", l)], [T("wu")], "wA")
                DMA("sp", wo, wbf["mix_w_o"][l].rearrange("(k p) n -> p k n", p=128), [("wbf", "mix_w_o", l)], [T("wo")], "wB")
                MS("pool", hh[:, :, 0:32], 0.0, [], [T("hh", c) for c in range(4)])
                deferred = []
                xbt = [T("xb", c) for c in range(8)]

                def stA1(b):
                    t0 = b * BT
                    for c in range(8):
                        CP(XBENG[c % 2], xb[:, c, :], resid[:, c, t0:t0 + BT], [RT(b, c)], [T("xb", c)])

                def stA2(b):
                    if b == 0:
                        for c in range(4):
                            for tp in range(31):
                                TS("pool" if c < 2 else "act", dg[c][:, tp, :], identb, pc(l, 78 + c * 31 + tp), None, ALU.mult, None,
                                   ["identb", "pcols"], [T("dg", c)])
                    if b > 0:
                        for c in range(4):
                            CP("pool", hh[:, c, 2:32], hh[:, c, 514:544], [T("hh", c)], [T("hh", c)])
                    for c in range(4):
                        bg_, ba_ = (c % 3) * 2, (c % 3) * 2 + 1
                        for k in range(8):
                            MM(ps[bg_][:, :], wu[:, k, 512 + c * 128:512 + (c + 1) * 128], xb[:, k, :], k == 0, k == 7,
                               [T("wu"), xbt[k]], PST[bg_])
                        for k in range(8):
                            MM(ps[ba_][:, :], wu[:, k, c * 128:(c + 1) * 128], xb[:, k, :], k == 0, k == 7,
                               [T("wu"), xbt[k]], PST[ba_])
                        ACT(sg[c % 2], ps[bg_][:, :], AF.Sigmoid, [PST[bg_], "pcols"], [T("sg")], bias=pc(l, 10 + c))
                        STT(hh[:, c, 32:544], ps[ba_][:, :], pc(l, 6 + c), sg[c % 2], ALU.add, ALU.mult,
                            [PST[ba_], T("sg"), "pcols"], [T("hh", c)])

                def stB(b):
                    for c in range(4):
                        d = dg[c]
                        bk = (4, 5, 0, 1)[c]
                        for tp in range(31):
                            MM(ps[bk][:, :], d[:, tp, :], hh[:, c, 2 + tp:2 + tp + BT], tp == 0, tp == 30,
                               [T("dg", c), T("hh", c)], PST[bk])
                        ACT(cv[:, c, :], ps[bk][:, :], AF.Identity, [PST[bk], "pcols"], [T("cv", c)], bias=pc(l, 18 + c))

                def stC(b, phase):
                    layer_norm(T, lambda c: cv[:, c, :], lambda c: [T("cv", c)], 4, ones_c, LN_EPS,
                               lambda c: pc(l, 22 + c), lambda c: pc(l, 26 + c), lambda c: ycv[:, c, :],
                               lambda c: [T("ycv", c)], lt, 6, 7, silu=True, phase=phase)

                def stD(b):
                    t0 = b * BT
                    for o in range(8):
                        bk = o % 4
                        for k in range(8):
                            rhs = yatt[:, k, t0:t0 + BT] if k < 4 else ycv[:, k - 4, :]
                            rt = ("yatt", b, k) if k < 4 else T("ycv", k - 4)
                            MM(ps[bk][:, :], wo[:, k, o * 128:(o + 1) * 128], rhs, k == 0, k == 7, [T("wo"), rt], PST[bk])
                        STT(resid[:, o, t0:t0 + BT], resid[:, o, t0:t0 + BT], ALPHA, ps[bk][:, :], ALU.mult, ALU.add,
                            [PST[bk], RT(b, o)], [RT(b, o)])

                def stL(b, phase):
                    t0 = b * BT
                    layer_norm(T, lambda c: resid[:, c, t0:t0 + BT], lambda c: [RT(b, c)], 8, ones_d, LN_EPS,
                               lambda c: pc(l, 30 + c), lambda c: pc(l, 38 + c), lambda c: resid[:, c, t0:t0 + BT],
                               lambda c: [RT(b, c)], lt, 6, 7, phase=phase, pool_chain=(b < NBLK - 1))

                stA1(0)
                stA2(0)
                stA1(1)
                stB(0)
                stC(0, 1)
                for b in range(NBLK):
                    nxt = b + 1 < NBLK
                    stC(b, 2)
                    if nxt:
                        stA2(b + 1)
                    stD(b)
                    stL(b, 1)
                    if b + 2 < NBLK:
                        stA1(b + 2)
                    stL(b, 2)
                    if nxt:
                        stB(b + 1)
                        stC(b + 1, 1)
                if stop_after == (l, 2):
                    break

                T = newsweep()
                wq = A.bf(8 * 1024).rearrange("p (k n) -> p k n", k=8)
                wo = A.bf(8 * 1024).rearrange("p (k n) -> p k n", k=8)
                kx = A.bf(8 * MEM).rearrange("p (c t) -> p c t", c=8)
                vx = A.bf(2 * 1024).rearrange("p (m n) -> p m n", m=2)
                xb = A.bf(8 * 512).rearrange("p (c t) -> p c t", c=8)
                qx = A.bf(8 * 512).rearrange("p (c t) -> p c t", c=8)
                yx = A.bf(8 * 512).rearrange("p (c t) -> p c t", c=8)
                exps = [A.bf(512) for _ in range(4)]
                rcp = [A.f32(512) for _ in range(2)]
                lt = ln_tmp(8)
                wkvx2d = A.bf(8 * 2048)
                wkvx = wkvx2d.rearrange("p (k n) -> p k n", k=8)
                qxs = [qx, wkvx2d[:, 0:4096].rearrange("p (c t) -> p c t", c=8)]
                yxs = [yx, wkvx2d[:, 4096:8192].rearrange("p (c t) -> p c t", c=8)]
                MS("pool", lt["eps"], LN_EPS, [], [T("eps")])
                if s == 0 and RL_ENG != "none":
                    srci = wbf["ffn_w_in"][l].rearrange("(k p) n -> p k n", p=128)
                    for g in range(NHC // 2):
                        dsti = wrl["ffn_w_in"][l, g].rearrange("p (k n) -> p k n", k=8)
                        DMA(RL_ENG, dsti[:, :, 0:256], srci[:, :, g * 256:(g + 1) * 256], [("wbf", "ffn_w_in", l)],
                            [("wrl", "ffn_w_in", l)], f"rl_{l}", grouped=True)
                        DMA(RL_ENG, dsti[:, :, 256:512], srci[:, :, FFN_H + g * 256:FFN_H + (g + 1) * 256], [("wbf", "ffn_w_in", l)],
                            [("wrl", "ffn_w_in", l)], f"rl_{l}", grouped=True)
                    srcd = wbf["ffn_w_down"][l].rearrange("(j p) (o n) -> o p j n", p=128, n=128)
                    for o in range(8):
                        dstd = wrl["ffn_w_down"][l, o].rearrange("p (j n) -> p j n", j=NHC)
                        DMA(RL_ENG, dstd, srcd[o], [("wbf", "ffn_w_down", l)], [("wrl", "ffn_w_down", l)], f"rl_{l}", grouped=True, nonc=False)
                DMA("sp", wkvx, wbf["xa_w_kv"][l].rearrange("(k p) n -> p k n", p=128), [("wbf", "xa_w_kv", l)], [T("wkvx")], "wA")
                DMA("sp", wq, wbf["xa_w_q"][l].rearrange("(k p) n -> p k n", p=128), [("wbf", "xa_w_q", l)], [T("wq")], "wB")
                DMA("sp", wo, wbf["xa_w_o"][l].rearrange("(k p) n -> p k n", p=128), [("wbf", "xa_w_o", l)], [T("wo")], "wC")
                for oc in range(8):
                    bk = oc % 4
                    for k in range(8):
                        MM(ps[bk][:, 0:MEM], wkvx[:, k, oc * 128:(oc + 1) * 128], memT[:, k, :], k == 0, k == 7,
                           [T("wkvx"), "memT"], PST[bk])
                    CP("act", kx[:, oc, :], ps[bk][:, 0:MEM], [PST[bk]], [T("kx")])
                for mt in range(2):
                    for hv in range(2):
                        bk = 4 + (2 * mt + hv)
                        for k in range(8):
                            MM(ps[bk][:, :], memT[:, k, mt * 128:(mt + 1) * 128], wkvx[:, k, 1024 + hv * 512:1024 + (hv + 1) * 512],
                               k == 0, k == 7, [T("wkvx"), "memT"], PST[bk])
                        CP("dve", vx[:, mt, hv * 512:(hv + 1) * 512], ps[bk][:, :], [PST[bk]], [T("vx")])
                xbt = [T("xb", c) for c in range(8)]

                def xA1(b):
                    t0 = b * BT
                    for c in range(8):
                        CP(XBENG[c % 2], xb[:, c, :], resid[:, c, t0:t0 + BT], [RT(b, c)], [T("xb", c)])

                def xA2(b):
                    q_ = qxs[b % 2]
                    for o in range(8):
                        bk = 6 + o % 2
                        for k in range(8):
                            MM(ps[bk][:, :], wq[:, k, o * 128:(o + 1) * 128], xb[:, k, :], k == 0, k == 7, [T("wq"), xbt[k]], PST[bk])
                        CP("act" if o % 2 == 0 else "dve", q_[:, o, :], ps[bk][:, :], [PST[bk]], [T("qx", b % 2, o), T("wkvx")])

                def xs(b, h):
                    q_ = qxs[b % 2]
                    for mt in range(2):
                        bk = (h % 2) * 2 + mt
                        for dc in range(2):
                            MM(ps[bk][:, :], kx[:, 2 * h + dc, mt * 128:(mt + 1) * 128], q_[:, 2 * h + dc, :], dc == 0, dc == 1,
                               [T("kx"), T("qx", b % 2, 2 * h + dc)], PST[bk])
                        ei = (2 * h + mt) % 4
                        ACT(exps[ei], ps[bk][:, :], AF.Exp, [PST[bk]], [T("e", ei)], scale=1.0 / 16.0)

                def xpv(b, h):
                    y_ = yxs[b % 2]
                    for dvc in range(2):
                        bk = 4 + dvc
                        for mt in range(2):
                            ei = (2 * h + mt) % 4
                            MM(ps[bk][:, :], vx[:, mt, (2 * h + dvc) * 128:(2 * h + dvc + 1) * 128], exps[ei], mt == 0, mt == 1,
                               [T("vx"), T("e", ei)], PST[bk])
                    bs = 6 + (h % 2)
                    for mt in range(2):
                        ei = (2 * h + mt) % 4
                        MM(ps[bs][:, :], ones_s, exps[ei], mt == 0, mt == 1, ["ones_s", T("e", ei)], PST[bs])
                    rc = rcp[h % 2]
                    ACT(rc, ps[bs][:, :], AF.Ln, [PST[bs]], [T("rc", h % 2)])
                    ACT(rc, rc, AF.Exp, [T("rc", h % 2)], [T("rc", h % 2)], scale=-1.0)
                    for dvc in range(2):
                        TT("dve", y_[:, 2 * h + dvc, :], ps[4 + dvc][:, :], rc, ALU.mult, [PST[4 + dvc], T("rc", h % 2)],
                           [T("yx", b % 2, 2 * h + dvc), T("wkvx")])

                def xD(b):
                    t0 = b * BT
                    y_ = yxs[b % 2]
                    for o in range(8):
                        bk = o % 2
                        for k in range(8):
                            MM(ps[bk][:, :], wo[:, k, o * 128:(o + 1) * 128], y_[:, k, :], k == 0, k == 7,
                               [T("wo"), T("yx", b % 2, k)], PST[bk])
                        STT(resid[:, o, t0:t0 + BT], resid[:, o, t0:t0 + BT], ALPHA, ps[bk][:, :], ALU.mult, ALU.add,
                            [PST[bk], RT(b, o)], [RT(b, o)])

                def xL(b, phase):
                    t0 = b * BT
                    layer_norm(T, lambda c: resid[:, c, t0:t0 + BT], lambda c: [RT(b, c)], 8, ones_d, LN_EPS,
                               lambda c: pc(l, 46 + c), lambda c: pc(l, 54 + c), lambda c: resid[:, c, t0:t0 + BT],
                               lambda c: [RT(b, c)], lt, 2, 3, phase=phase, pool_chain=(b < NBLK - 1))

                xA1(0)
                xA2(0)
                xA1(1)
                for b in range(NBLK):
                    xs(b, 0)
                    xs(b, 1)
                    if b + 1 < NBLK:
                        xA2(b + 1)
                    if b > 0:
                        xL(b - 1, 2)
                    xpv(b, 0)
                    xs(b, 2)
                    xpv(b, 1)
                    xs(b, 3)
                    xpv(b, 2)
                    xpv(b, 3)
                    xD(b)
                    xL(b, 1)
                    if b + 2 < NBLK:
                        xA1(b + 2)
                xL(NBLK - 1, 2)
                if stop_after == (l, 3):
                    break

                T = newsweep()
                xb2 = A.bf(8 * 1024).rearrange("p (c t) -> p c t", c=8)
                hT = A.bf(NHC * 1024).rearrange("p (c t) -> p c t", c=NHC)
                wring2 = [A.bf(8 * 512) for _ in range(3)]
                dring2 = [A.bf(NHC * 128) for _ in range(2)]
                wring = [w.rearrange("p (k n) -> p k n", k=8) for w in wring2]
                dring = [w.rearrange("p (k n) -> p k n", k=NHC) for w in dring2]
                sl = [A.f32(512) for _ in range(3)]
                lt = ln_tmp(8)
                ots = [A.f32(1024), A.f32(1024)]
                MS("pool", lt["eps"], LN_EPS, [], [T("eps")])
                wfi = wbf["ffn_w_in"][l].rearrange("(k p) n -> p k n", p=128)
                wfd = wbf["ffn_w_down"][l].rearrange("(k p) n -> p k n", p=128)
                last = (l == depth - 1)
                n_sl = 0
                def xb2_cast(hf):
                    for c in range(8):
                        CP("act" if c % 2 == 0 else "dve", xb2[:, c, :], resid[:, c, hf * 1024:(hf + 1) * 1024],
                           [RT(2 * hf, c), RT(2 * hf + 1, c)], [T("xb2", c)])

                def ln3(phase, b):
                    t0 = b * BT
                    layer_norm(T, lambda c: resid[:, c, t0:t0 + BT], lambda c: [RT(b, c)], 8, ones_d, LN_EPS,
                               lambda c: pc(l, 62 + c), lambda c: pc(l, 70 + c), lambda c: resid[:, c, t0:t0 + BT],
                               lambda c: [RT(b, c)], lt, 6, 7, phase=phase, pool_chain=(b < 2))

                def emit_out(b, banks=(4, 5)):
                    t0 = b * BT
                    for tt in range(4):
                        oi = n_ot[0] % 2
                        ot = ots[oi]
                        n_ot[0] += 1
                        for half in range(2):
                            bk = banks[(2 * tt + half) % len(banks)]
                            for j in range(4):
                                c = half * 4 + j
                                TR(ps[bk][:, j * 128:(j + 1) * 128], resid[:, c, t0 + tt * 128:t0 + (tt + 1) * 128], ident_f,
                                   [RT(b, c), "ident"], PST[bk])
                            CP("dve" if half == 0 else "act", ot[:, half * 512:(half + 1) * 512], ps[bk][:, :], [PST[bk]],
                               [T("ot", oi, half)])
                        outtoks.append(("outdram", len(outtoks)))
                        DMA("sp", out_d[s, t0 + tt * 128:t0 + (tt + 1) * 128, :], ot, [T("ot", oi, 0), T("ot", oi, 1)],
                            [outtoks[-1]], f"ot{oi}")

                def finish(b):
                    ln3(2, b)
                    if last:
                        emit_out(b, banks=(0, 1, 2, 3, 4, 5))

                n_ot = [0]
                xbt = [T("xb2", c) for c in range(8)]
                pending = []
                xb2_cast(0)
                for hf in range(2):
                    for g in range(NHC // 2):
                        ri = (hf * (NHC // 2) + g) % 3
                        wr = wring[ri]
                        DMA("sp", wr[:, :, 0:256], wfi[:, :, g * 256:(g + 1) * 256], [("wbf", "ffn_w_in", l)], [T("wr", ri)], f"wr{ri}a")
                        DMA("sp", wr[:, :, 256:512], wfi[:, :, FFN_H + g * 256:FFN_H + (g + 1) * 256], [("wbf", "ffn_w_in", l)],
                            [T("wr", ri)], f"wr{ri}b")
                        for bb in range(2):
                            for cc in range(2):
                                j = 2 * g + cc
                                bg_, bu_ = 2 * ((2 * bb + cc) % 2), 2 * ((2 * bb + cc) % 2) + 1
                                for k in range(8):
                                    MM(ps[bg_][:, :], wr[:, k, cc * 128:(cc + 1) * 128], xb2[:, k, bb * 512:(bb + 1) * 512],
                                       k == 0, k == 7, [T("wr", ri), xbt[k]], PST[bg_])
                                for k in range(8):
                                    MM(ps[bu_][:, :], wr[:, k, 256 + cc * 128:256 + (cc + 1) * 128], xb2[:, k, bb * 512:(bb + 1) * 512],
                                       k == 0, k == 7, [T("wr", ri), xbt[k]], PST[bu_])
                                si = n_sl % 3
                                n_sl += 1
                                ACT(sl[si], ps[bg_][:, :], AF.Silu, [PST[bg_]], [T("sl", si)])
                                TT("dve", hT[:, j, bb * 512:(bb + 1) * 512], ps[bu_][:, :], sl[si], ALU.mult,
                                   [PST[bu_], T("sl", si)], [T("hT", j, bb)])
                        if g in (1, 3, 5) and pending:
                            pending.pop(0)()
                    while pending:
                        pending.pop(0)()
                    if hf == 0:
                        xb2_cast(1)
                    for o in range(8):
                        di = (hf * 8 + o) % 2
                        dr = dring[di]
                        DMA("sp", dr, wfd[:, :, o * 128:(o + 1) * 128], [("wbf", "ffn_w_down", l)], [T("dr", di)], f"dr{di}")
                        for bb in range(2):
                            b = 2 * hf + bb
                            bk = 4 + (2 * o + bb) % 4
                            for j in range(NHC):
                                MM(ps[bk][:, :], dr[:, j, :], hT[:, j, bb * 512:(bb + 1) * 512], j == 0, j == NHC - 1,
                                   [T("dr", di), T("hT", j, bb)], PST[bk])
                            STT(resid[:, o, b * BT:(b + 1) * BT], resid[:, o, b * BT:(b + 1) * BT], ALPHA, ps[bk][:, :],
                                ALU.mult, ALU.add, [PST[bk], RT(b, o)], [RT(b, o)])
                    b0, b1 = 2 * hf, 2 * hf + 1
                    ln3(1, b0)
                    if hf == 0:
                        def item0(b0=b0, b1=b1):
                            ln3(2, b0)
                            ln3(1, b1)

                        def item1(b0=b0, b1=b1):
                            if last:
                                emit_out(b0)
                            ln3(2, b1)

                        def item2(b1=b1):
                            if last:
                                emit_out(b1)
                        pending.append(item0)
                        pending.append(item1)
                        pending.append(item2)
                    else:
                        finish(b0)
                        ln3(1, b1)
                        finish(b1)
                if stop_after == (l, 4):
                    break
            if stop_after is not None:
                break
        if stop_after is not None:
            P.barrier()
            if stop_after[1] == 1:
                for c in range(4):
                    CP("dve", resid[:, c, :], yatt[:, c, :], [("yatt", bb, c) for bb in range(4)], [RT(bb, c) for bb in range(4)])
            dv = out_d[0].rearrange("(a b) d -> a (b d)", a=1024)
            for c in range(8):
                outtoks.append(("outdram", len(outtoks)))
                DMA("sp", dv[c * 128:(c + 1) * 128, :], resid[:, c, :], [RT(bb, c) for bb in range(4)], [outtoks[-1]], "dbg")
        P.barrier()
        P.op("sp", lambda e: e.nop(), list(outtoks), [])
        P.emit(st)
        build.info = dict(n_ops=len(P.ops), n_sems=P.n_sems, peak=A.peak)
    return nc


def _host_layout(inp):
    L = DEPTH
    pcols = np.zeros((128, L * NCOL), np.float32)

    def put(l, j, vec):
        v = np.asarray(vec, np.float32).reshape(-1, 128)
        for i in range(v.shape[0]):
            pcols[:, l * NCOL + j + i] = v[i]
    for l in range(L):
        b_in = np.asarray(inp["mix_b_in"][l], np.float32)
        put(l, 0, b_in[0:512])
        pcols[64:96, l * NCOL + 4] = b_in[512:544]
        pcols[64:80, l * NCOL + 5] = b_in[528:544]
        pcols[80:96, l * NCOL + 5] = b_in[512:528]
        put(l, 6, b_in[544:1056])
        put(l, 10, b_in[1056:1568])
        put(l, 14, inp["mla_q_norm"][l])
        put(l, 16, inp["mla_kv_norm"][l])
        put(l, 18, inp["conv_dw_b"][l])
        put(l, 22, inp["conv_norm_g"][l])
        put(l, 26, inp["conv_norm_b"][l])
        put(l, 30, inp["ln1_g"][l]); put(l, 38, inp["ln1_b"][l])
        put(l, 46, inp["ln2_g"][l]); put(l, 54, inp["ln2_b"][l])
        put(l, 62, inp["ln3_g"][l]); put(l, 70, inp["ln3_b"][l])
        dw = np.asarray(inp["conv_dw_w"][l], np.float32)
        for c in range(4):
            pcols[:, l * NCOL + 78 + c * 31:l * NCOL + 78 + (c + 1) * 31] = dw[:, c * 128:(c + 1) * 128].T
    cvec = np.zeros((128, 4), np.float32)
    inv_freq = (np.float32(10000.0) ** (-np.arange(0, 32, 2, dtype=np.float32) / np.float32(32))).astype(np.float32)
    for p in range(128):
        cvec[p, 0] = inv_freq[p % 16]
        cvec[p, 1] = -1.0 if (p % 32) < 16 else 1.0
    return pcols, cvec


_NC_CACHE = {}


def kernel(**inputs):
    nseq = 32 // N_CORES
    if "nc" not in _NC_CACHE:
        _NC_CACHE["nc"] = build(nseq=nseq, depth=DEPTH)
    nc = _NC_CACHE["nc"]
    pcols, cvec = _host_layout(inputs)
    ident = np.eye(128, dtype=np.float32)
    x = np.ascontiguousarray(np.asarray(inputs["x"], np.float32))
    mem = np.ascontiguousarray(np.asarray(inputs["mem"], np.float32))
    pos = np.ascontiguousarray(np.asarray(inputs["positions"], np.int32))
    shared = {n: np.ascontiguousarray(np.asarray(inputs[n], np.float32)) for n in WNAMES}
    in_maps = []
    for i in range(N_CORES):
        m = dict(shared)
        m["x"] = x[i * nseq:(i + 1) * nseq]
        m["mem"] = mem[i * nseq:(i + 1) * nseq]
        m["positions"] = pos[i * nseq:(i + 1) * nseq]
        m["pcols"] = pcols
        m["cvec"] = cvec
        m["ident"] = ident
        in_maps.append(m)
    res = run_bass_kernel_spmd(nc, in_maps, core_ids=list(range(N_CORES)))
    return np.concatenate([np.asarray(r["out"], np.float32) for r in res.results], axis=0)
```

```python
import contextlib
import math
import numpy as np
import concourse.bass as bass
import concourse.mybir as mybir
from concourse.bass_utils import run_bass_kernel_spmd

F32 = mybir.dt.float32
BF16 = mybir.dt.bfloat16
I32 = mybir.dt.int32
AF = mybir.ActivationFunctionType
ALU = mybir.AluOpType

D = 1024
SEQ = 2048
MEM = 256
DEPTH = 2
BT = 512
NBLK = SEQ // BT
NH = 8
IN_COLS = 1568
FFN_H = 2816
NHC = FFN_H // 128
ALPHA = (2.0 * DEPTH) ** 0.25
LN_EPS = 1e-5
RMS_EPS = 1e-6
NCOL = 208
N_CORES = 8
RL_ENG = "none"
DG_ENG = "act"
XB23 = ("act", "dve")
SEM_CAP = 16000
TWO_PI = 2.0 * math.pi
CW1 = 6.28125
CW2 = TWO_PI - CW1
PI_SAFE = 3.14159

WNAMES = ["mix_w_in", "mla_w_uq", "mla_w_ukv", "mix_w_o", "xa_w_kv", "xa_w_q", "xa_w_o",
          "ffn_w_in", "ffn_w_down"]
WSHAPES = {"mix_w_in": (1024, 1568), "mla_w_uq": (256, 768), "mla_w_ukv": (256, 1024),
           "mix_w_o": (1024, 1024), "xa_w_kv": (1024, 2048), "xa_w_q": (1024, 1024),
           "xa_w_o": (1024, 1024), "ffn_w_in": (1024, 5632), "ffn_w_down": (2816, 1024)}


class Prog:
    ENGS = ("pe", "act", "dve", "pool", "sp")

    def __init__(self, nc):
        self.nc = nc
        self.ops = []
        self.last_w = {}
        self.readers = {}
        self.dma_slots = {}
        self.eng_ops = {e: [] for e in self.ENGS}
        self.bar = {e: set() for e in self.ENGS}

    def op(self, eng, fn, reads=(), writes=(), dma_slot=None, grouped=False):
        i = len(self.ops)
        deps = set(self.bar[eng])
        self.bar[eng] = set()
        for t in reads:
            w = self.last_w.get(t)
            if w is not None:
                deps.add(w)
        for t in writes:
            w = self.last_w.get(t)
            if w is not None:
                deps.add(w)
            deps.update(self.readers.get(t, ()))
        rec = dict(id=i, eng=eng, fn=fn, deps=deps, dma=dma_slot, ms=None, dmaval=None)
        if dma_slot is not None:
            st = self.dma_slots.setdefault(dma_slot, dict(count=0, last=None, grouped=grouped))
            if st["last"] is not None and not grouped:
                deps.add(st["last"])
            st["count"] += 1
            st["last"] = i
            rec["dmaval"] = st["count"]
        deps.discard(i)
        self.ops.append(rec)
        self.eng_ops[eng].append(rec)
        for t in reads:
            self.readers.setdefault(t, []).append(i)
        for t in writes:
            self.last_w[t] = i
            self.readers[t] = []
        return i

    def barrier(self):
        last = set()
        for e in self.ENGS:
            if self.eng_ops[e]:
                last.add(self.eng_ops[e][-1]["id"])
        for st in self.dma_slots.values():
            if st["last"] is not None:
                last.add(st["last"])
        for e in self.ENGS:
            self.bar[e] |= last

    def emit(self, stack):
        nc = self.nc
        ops = self.ops
        needed = set()
        for r in ops:
            for d in r["deps"]:
                p = ops[d]
                if p["dma"] is not None:
                    continue
                if p["eng"] == "pe" and r["eng"] == "pe" and r["dma"] is None:
                    continue
                needed.add(d)
        cnt = {e: 0 for e in self.ENGS}
        for r in ops:
            if r["id"] in needed:
                cnt[r["eng"]] += 1
                r["ms"] = cnt[r["eng"]]
        esems = {}
        for e in self.ENGS:
            n = max(1, (cnt[e] + SEM_CAP - 1) // SEM_CAP)
            esems[e] = [stack.enter_context(nc.semaphore(f"s_{e}{k}")) for k in range(n)]
        per = SEM_CAP // 16
        dsems = {}
        for s, st in self.dma_slots.items():
            n = max(1, (st["count"] + per - 1) // per)
            j = len(dsems)
            dsems[s] = [stack.enter_context(nc.semaphore(f"d{j}_{k}")) for k in range(n)]
        self.n_sems = sum(len(v) for v in esems.values()) + sum(len(v) for v in dsems.values())

        def run_engine(ename):
            def body(eng):
                waited = {}
                for r in self.eng_ops[ename]:
                    reqs = {}
                    for d in r["deps"]:
                        p = ops[d]
                        if p["dma"] is not None:
                            dv = self.dma_slots[p["dma"]]["count"] if self.dma_slots[p["dma"]]["grouped"] else p["dmaval"]
                            k = (dv - 1) // per
                            v = ((dv - 1) % per + 1) * 16
                            key = ("d", p["dma"], k)
                            sem = dsems[p["dma"]][k]
                        else:
                            if p["eng"] == "pe" and ename == "pe" and r["dma"] is None:
                                continue
                            m = p["ms"]
                            k = (m - 1) // SEM_CAP
                            v = (m - 1) % SEM_CAP + 1
                            key = ("e", p["eng"], k)
                            sem = esems[p["eng"]][k]
                        if v > reqs.get(key, (0, None))[0]:
                            reqs[key] = (v, sem)
                    for key, (v, sem) in reqs.items():
                        if waited.get(key, 0) >= v:
                            continue
                        waited[key] = v
                        eng.wait_ge(sem, v)
                    ins = r["fn"](eng)
                    if r["dma"] is not None:
                        k = (r["dmaval"] - 1) // per
                        ins.then_inc(dsems[r["dma"]][k], 16)
                    elif r["ms"] is not None:
                        k = (r["ms"] - 1) // SEM_CAP
                        ins.then_inc(esems[ename][k], 1)
            return body

        with nc.Block() as block:
            block.tensor(run_engine("pe"))
            block.scalar(run_engine("act"))
            block.vector(run_engine("dve"))
            block.gpsimd(run_engine("pool"))
            block.sync(run_engine("sp"))


class Arena:
    def __init__(self, t, total):
        self.t, self.total, self.off, self.base = t, total, 0, 0
        self.peak = 0

    def _take(self, n):
        a = self.off
        self.off += n
        self.peak = max(self.peak, self.off)
        assert self.off <= self.total, f"SBUF arena overflow {self.off} > {self.total}"
        return self.t[:, a:a + n]

    def f32(self, n):
        return self._take(n)

    def bf(self, n):
        assert n % 2 == 0
        return self._take(n // 2).bitcast(BF16)

    def i32(self, n):
        return self._take(n).bitcast(I32)

    def mark(self):
        self.base = self.off

    def reset(self):
        self.off = self.base


def build(nseq=4, depth=DEPTH, stop_after=None):
    nc = bass.Bass("TRN2", target_bir_lowering=False)
    x_d = nc.dram_tensor("x", [nseq, SEQ, D], F32, kind="ExternalInput").ap()
    mem_d = nc.dram_tensor("mem", [nseq, MEM, D], F32, kind="ExternalInput").ap()
    pos_d = nc.dram_tensor("positions", [nseq, SEQ], I32, kind="ExternalInput").ap()
    pcols_d = nc.dram_tensor("pcols", [128, DEPTH * NCOL], F32, kind="ExternalInput").ap()
    cvec_d = nc.dram_tensor("cvec", [128, 4], F32, kind="ExternalInput").ap()
    ident_d = nc.dram_tensor("ident", [128, 128], F32, kind="ExternalInput").ap()
    w32, wbf, wrl = {}, {}, {}
    for n in WNAMES:
        r, c = WSHAPES[n]
        w32[n] = nc.dram_tensor(n, [DEPTH, r, c], F32, kind="ExternalInput").ap()
        wbf[n] = nc.dram_tensor(n + "_bf", [DEPTH, r, c], BF16, kind="Internal").ap()
        if n == "ffn_w_in":
            wrl[n] = nc.dram_tensor(n + "_rl", [DEPTH, NHC // 2, 128, 8 * 512], BF16, kind="Internal").ap()
        elif n == "ffn_w_down":
            wrl[n] = nc.dram_tensor(n + "_rl", [DEPTH, 8, 128, NHC * 128], BF16, kind="Internal").ap()
    out_d = nc.dram_tensor("out", [nseq, SEQ, D], F32, kind="ExternalOutput").ap()

    with contextlib.ExitStack() as st:
        TOTAL = 53100
        arena_t = st.enter_context(nc.sbuf_tensor("arena", [128, TOTAL], F32))
        ps = [st.enter_context(nc.psum_tensor(f"ps{i}", [128, 512], F32)) for i in range(8)]
        PST = [f"ps{i}" for i in range(8)]
        A = Arena(arena_t, TOTAL)
        P = Prog(nc)

        def MM(out, lhsT, rhs, start, stop, reads, wtok):
            P.op("pe", lambda e: e.matmul(out, lhsT=lhsT, rhs=rhs, start=start, stop=stop), reads, [wtok])

        def TR(out, in_, ident, reads, wtok):
            P.op("pe", lambda e: e.transpose(out=out, in_=in_, identity=ident), reads, [wtok])

        def ACT(out, in_, func, reads, writes, scale=None, bias=None):
            kw = {}
            if scale is not None:
                kw["scale"] = scale
            if bias is not None:
                kw["bias"] = bias
            P.op("act", lambda e: e.activation(out=out, in_=in_, func=func, **kw), reads, writes)

        def TT(eng, out, in0, in1, op, reads, writes):
            P.op(eng, lambda e: e.tensor_tensor(out=out, in0=in0, in1=in1, op=op), reads, writes)

        def TS(eng, out, in0, s1, s2, op0, op1, reads, writes):
            if eng == "act":
                assert op1 is None and op0 == ALU.mult
                P.op("act", lambda e: e.activation(out=out, in_=in0, func=AF.Copy, scale=s1), reads, writes)
            elif op1 is None and eng == "pool" and op0 == ALU.mult:
                P.op(eng, lambda e: e.tensor_scalar(out=out, in0=in0, scalar1=s1, scalar2=0.0, op0=ALU.mult, op1=ALU.add), reads, writes)
            elif op1 is None:
                P.op(eng, lambda e: e.tensor_scalar(out=out, in0=in0, scalar1=s1, scalar2=None, op0=op0), reads, writes)
            else:
                P.op(eng, lambda e: e.tensor_scalar(out=out, in0=in0, scalar1=s1, scalar2=s2, op0=op0, op1=op1), reads, writes)

        def STT(out, in0, scalar, in1, op0, op1, reads, writes):
            P.op("dve", lambda e: e.scalar_tensor_tensor(out=out, in0=in0, scalar=scalar, in1=in1, op0=op0, op1=op1), reads, writes)

        def CP(eng, out, in_, reads, writes):
            if eng == "act":
                P.op("act", lambda e: e.activation(out=out, in_=in_, func=AF.Copy), reads, writes)
            else:
                P.op(eng, lambda e: e.tensor_copy(out=out, in_=in_), reads, writes)

        def MS(eng, ap, val, reads, writes):
            P.op(eng, lambda e: e.memset(ap, val), reads, writes)

        def DMA(eng, out, in_, reads, writes, slot, nonc=False, grouped=False):
            if grouped:
                P.op(eng, lambda e: e.dma_start(out=out, in_=in_), reads, writes, dma_slot=slot, grouped=True)
            elif nonc:
                def f(e):
                    with nc.allow_non_contiguous_dma(reason="small strided layout load"):
                        return e.dma_start(out=out, in_=in_)
                P.op(eng, f, reads, writes, dma_slot=slot)
            else:
                P.op(eng, lambda e: e.dma_start(out=out, in_=in_), reads, writes, dma_slot=slot)

        resid = A.f32(8 * SEQ).rearrange("p (c t) -> p c t", c=8)
        ident_f = A.f32(128)
        ones_s = A.bf(128)
        ones_d = A.bf(128)
        ones_c = A.bf(128)
        ones_r = A.bf(128)
        identb = A.bf(128)
        cvec = A.f32(4)
        pcols = A.f32(DEPTH * NCOL)
        memT = A.bf(8 * MEM).rearrange("p (c t) -> p c t", c=8)
        A.mark()

        def RT(b, c):
            return ("resid", b, c)

        def RTA(b):
            return [("resid", b, c) for c in range(8)]

        DMA("sp", ident_f, ident_d, [], ["ident"], "c0")
        DMA("sp", cvec, cvec_d, [], ["cvec"], "c1")
        DMA("sp", pcols, pcols_d, [], ["pcols"], "c2")
        for l in range(depth):
            for n in WNAMES:
                r, c = WSHAPES[n]
                k = c if c <= 1568 else (c // 2 if c // 2 <= 1568 else c // 4)
                src = w32[n][l].rearrange("r (a k) -> (r a) k", k=k)
                dst = wbf[n][l].rearrange("r (a k) -> (r a) k", k=k)
                DMA("pool", dst, src, [], [("wbf", n, l)], f"cast_{n}_{l}")
        MS("pool", ones_s, 1.0, [], ["ones_s"])
        MS("pool", ones_d, 1.0 / 1024, [], ["ones_d"])
        MS("pool", ones_c, 1.0 / 512, [], ["ones_c"])
        MS("pool", ones_r, 1.0 / 256, [], ["ones_r"])
        CP("dve", identb, ident_f, ["ident"], ["identb"])
        for l in range(depth):
            o = l * NCOL
            TT("dve", pcols[:, o + 202:o + 206], pcols[:, o + 0:o + 4], pcols[:, o + 14:o + 18], ALU.mult,
               ["pcols"], ["pcols"])

        def pc(l, j):
            return pcols[:, l * NCOL + j:l * NCOL + j + 1]

        uid = [0]
        outtoks = []

        def newsweep():
            P.barrier()
            A.reset()
            uid[0] += 1
            u = uid[0]
            return lambda *a: (u,) + a

        def layer_norm(T, zsrc, ztoks, C, ones_m, eps, gcol, bcol, dst_fn, dtoks, tmp, psA, psB, silu=False, phase=0,
                       pool_chain=False):
            zb, zsq, t1, t2, t3 = tmp["zb"], tmp["zsq"], tmp["t1"], tmp["t2"], tmp["t3"]
            if phase in (0, 1):
                for c in range(C):
                    CP("dve", zb[:, c, :], zsrc(c), ztoks(c), [T("zb", c)])
                    ACT(zsq[:, c, :], zsrc(c), AF.Square, ztoks(c), [T("zsq", c)])
            if phase == 1:
                return
            for c in range(C):
                MM(ps[psA][:, :], ones_m, zb[:, c, :], c == 0, c == C - 1, [T("zb", c), "ones"], PST[psA])
            for c in range(C):
                MM(ps[psB][:, :], ones_m, zsq[:, c, :], c == 0, c == C - 1, [T("zsq", c), "ones"], PST[psB])
            ACT(t1, ps[psA][:, :], AF.Square, [PST[psA]], [T("t1")])
            TT("dve", t2, ps[psB][:, :], t1, ALU.subtract, [PST[psB], T("t1")], [T("t2")])
            ACT(t2, t2, AF.Ln, [T("t2"), T("eps")], [T("t2")], bias=tmp["eps"])
            ACT(t1, t2, AF.Exp, [T("t2"), T("t1")], [T("t1")], scale=-0.5)
            STT(t3, ps[psA][:, :], -1.0, t1, ALU.mult, ALU.mult, [PST[psA], T("t1")], [T("t3")])
            for c in range(C):
                z = zsrc(c)
                zt = tmp["zt"][:, c % 2, :]
                if pool_chain:
                    TT("pool", zt, z, t1, ALU.mult, ztoks(c) + [T("t1")], [T("zt", c % 2)])
                    TT("pool", zt, zt, t3, ALU.add, [T("zt", c % 2), T("t3")], [T("zt", c % 2)])
                    TS("pool", dst_fn(c), zt, gcol(c), bcol(c), ALU.mult, ALU.add, [T("zt", c % 2), "pcols"], dtoks(c))
                    continue
                TT("pool", zt, z, t1, ALU.mult, ztoks(c) + [T("t1")], [T("zt", c % 2)])
                TT("dve", zt, zt, t3, ALU.add, [T("zt", c % 2), T("t3")], [T("zt", c % 2)])
                ACT(dst_fn(c), zt, AF.Silu if silu else AF.Identity, [T("zt", c % 2), "pcols"], dtoks(c),
                    scale=gcol(c), bias=bcol(c))

        def ln_tmp(C):
            d = dict(zb=A.bf(C * 512).rearrange("p (c t) -> p c t", c=C),
                     zsq=A.bf(C * 512).rearrange("p (c t) -> p c t", c=C),
                     t1=A.f32(512), t2=A.f32(512), t3=A.f32(512),
                     zt=A.f32(1024).rearrange("p (c t) -> p c t", c=2), eps=A.f32(1))
            return d

        for s in range(nseq):
            T = newsweep()
            xts = [A.f32(1024), A.f32(1024)]
            n_t = 0
            for tt in range(SEQ // 128 + MEM // 128):
                is_mem = tt >= SEQ // 128
                xt = xts[tt % 2]
                src = mem_d[s, (tt - 16) * 128:(tt - 15) * 128, :] if is_mem else x_d[s, tt * 128:(tt + 1) * 128, :]
                DMA("sp", xt, src, [], [T("xt", tt % 2)], f"xt{tt % 2}")
                for half in range(2):
                    bk = n_t % 8
                    n_t += 1
                    for j in range(4):
                        c = half * 4 + j
                        TR(ps[bk][:, j * 128:(j + 1) * 128], xt[:, c * 128:(c + 1) * 128], ident_f,
                           [T("xt", tt % 2), "ident"], PST[bk])
                    src_v = ps[bk][:, :].rearrange("p (j t) -> p j t", j=4)
                    if is_mem:
                        mt = tt - 16
                        CP("act", memT[:, half * 4:half * 4 + 4, mt * 128:(mt + 1) * 128], src_v, [PST[bk]], ["memT"])
                    else:
                        CP("dve", resid[:, half * 4:half * 4 + 4, tt * 128:(tt + 1) * 128], src_v, [PST[bk]],
                           [RT(tt // 4, half * 4 + j) for j in range(4)])

            for l in range(depth):
                T = newsweep()
                XBENG = ("dve", "dve")
                w1 = A.bf(8 * 544).rearrange("p (k n) -> p k n", k=8)
                wks = A.bf(8 * 96).rearrange("p (k n) -> p k n", k=8)
                wqa = A.bf(2 * 768).rearrange("p (k n) -> p k n", k=2)
                wqb = A.bf(2 * 768).rearrange("p (k n) -> p k n", k=2)
                wkv = A.bf(2 * 1024).rearrange("p (k n) -> p k n", k=2)
                KT = A.bf(8 * SEQ).rearrange("p (h t) -> p h t", h=8)
                VV = A.bf(16 * 512).rearrange("p (k n) -> p k n", k=16)
                yatt = A.bf(4 * SEQ).rearrange("p (c t) -> p c t", c=4)
                xb = A.bf(8 * 512).rearrange("p (c t) -> p c t", c=8)
                cg = A.bf(4 * 512).rearrange("p (c t) -> p c t", c=4)
                csq = A.bf(4 * 512).rearrange("p (c t) -> p c t", c=4)
                rq = A.f32(512)
                srt = A.f32(512)
                rkv = A.f32(512)
                cosT = A.f32(512)
                sinT = A.f32(512)
                posi = A.i32(512)
                ang = A.f32(512)
                kf = A.f32(512)
                qT = A.bf(8 * 512).rearrange("p (h t) -> p h t", h=8)
                tA, tB = ang, kf
                krb = posi.bitcast(BF16)[:, 0:512]
                exps = [csq[:, i, :] for i in range(3)]
                rcp = [A.f32(512) for _ in range(2)]
                rtok = A.f32(8)
                epsr = A.f32(1)
                MS("pool", epsr, RMS_EPS, [], [T("epsr")])
                win = wbf["mix_w_in"][l].rearrange("(k p) n -> p k n", p=128)
                wt = ("wbf", "mix_w_in", l)
                DMA("sp", w1, win[:, :, 0:544], [wt], [T("w1")], "wA")
                DMA("sp", wks[:, :, 0:64], win[:, :, 448:512], [wt], [T("wks")], "wB")
                DMA("sp", wks[:, :, 64:80], win[:, :, 528:544], [wt], [T("wks")], "wB", nonc=True)
                DMA("sp", wks[:, :, 80:96], win[:, :, 512:528], [wt], [T("wks")], "wB", nonc=True)
                wuq = wbf["mla_w_uq"][l].rearrange("(k p) n -> p k n", p=128)
                wt = ("wbf", "mla_w_uq", l)
                DMA("sp", wqa, wuq, [wt], [T("wqa")], "wC")
                DMA("sp", wqb, wuq, [wt], [T("wqb")], "wD")
                wqb4 = wqb.rearrange("p k (h d) -> p k h d", h=8)
                wuq4 = wuq.rearrange("p k (h d) -> p k h d", h=8)
                for k in range(2):
                    DMA("sp", wqb4[:, k, :, 64:80], wuq4[:, k, :, 80:96], [wt], [T("wqb")], "wD", nonc=True)
                    DMA("sp", wqb4[:, k, :, 80:96], wuq4[:, k, :, 64:80], [wt], [T("wqb")], "wD", nonc=True)
                DMA("sp", wkv, wbf["mla_w_ukv"][l].rearrange("(k p) n -> p k n", p=128), [("wbf", "mla_w_ukv", l)],
                    [T("wkv")], "wE")
                wkv4 = wkv.rearrange("p k (h d) -> p k h d", h=8)
                sc = 96.0 ** -0.5
                for b in range(NBLK):
                    t0 = b * BT
                    for c in range(8):
                        CP(XBENG[c % 2], xb[:, c, :], resid[:, c, t0:t0 + BT], [RT(b, c)], [T("xb", c)])
                    xbt = [T("xb", c) for c in range(8)]
                    R = slice(64, 96)
                    DMA("sp", posi[R, :], pos_d[s:s + 1, t0:t0 + BT].partition_broadcast(32), [], [T("posi")], "pos")
                    CP("dve", ang[R, :], posi[R, :], [T("posi")], [T("ang")])
                    TS("dve", ang[R, :], ang[R, :], cvec[R, 0:1], None, ALU.mult, None, [T("ang"), "cvec"], [T("ang")])
                    TS("dve", posi[R, :], ang[R, :], 1.0 / TWO_PI, None, ALU.mult, None, [T("ang")], [T("posi")])
                    CP("dve", kf[R, :], posi[R, :], [T("posi")], [T("kf")])
                    STT(ang[R, :], kf[R, :], -CW1, ang[R, :], ALU.mult, ALU.add, [T("kf"), T("ang")], [T("ang")])
                    STT(ang[R, :], kf[R, :], -CW2, ang[R, :], ALU.mult, ALU.add, [T("kf"), T("ang")], [T("ang")])
                    TS("dve", kf[R, :], ang[R, :], math.pi / 2, None, ALU.add, None, [T("ang")], [T("kf")])
                    TS("dve", cosT[R, :], kf[R, :], math.pi, -TWO_PI, ALU.is_gt, ALU.mult, [T("kf")], [T("cosT")])
                    TT("dve", kf[R, :], kf[R, :], cosT[R, :], ALU.add, [T("kf"), T("cosT")], [T("kf")])
                    TS("dve", kf[R, :], kf[R, :], PI_SAFE, -PI_SAFE, ALU.min, ALU.max, [T("kf")], [T("kf")])
                    TS("dve", ang[R, :], ang[R, :], PI_SAFE, -PI_SAFE, ALU.min, ALU.max, [T("ang")], [T("ang")])
                    ACT(cosT[R, :], kf[R, :], AF.Sin, [T("kf")], [T("cosT")])
                    ACT(sinT[R, :], ang[R, :], AF.Sin, [T("ang")], [T("sinT")])
                    TS("dve", sinT[R, :], sinT[R, :], cvec[R, 1:2], None, ALU.mult, None, [T("sinT"), "cvec"], [T("sinT")])
                    for j in range(4):
                        bk = j
                        for k in range(8):
                            MM(ps[bk][:, :], w1[:, k, j * 128:(j + 1) * 128], xb[:, k, :], k == 0, k == 7,
                               [T("w1"), xbt[k]], PST[bk])
                        ACT(csq[:, j, :], ps[bk][:, :], AF.Square, [PST[bk], "pcols"], [T("csq", j)], bias=pc(l, j))
                        ACT(cg[:, j, :], ps[bk][:, :], AF.Identity, [PST[bk], "pcols"], [T("cg", j)],
                            scale=pc(l, 14 + j), bias=pc(l, 202 + j))
                    for k in range(8):
                        MM(ps[4][0:96, :], w1[:, k, 448:544], xb[:, k, :], k == 0, k == 7, [T("w1"), xbt[k]], PST[4])
                    for k in range(8):
                        MM(ps[5][0:96, :], wks[:, k, :], xb[:, k, :], k == 0, k == 7, [T("wks"), xbt[k]], PST[5])
                    ACT(tA[R, :], ps[4][R, :], AF.Identity, [PST[4], "pcols"], [T("ang")], bias=pcols[R, l * NCOL + 4:l * NCOL + 5])
                    ACT(tB[R, :], ps[5][R, :], AF.Identity, [PST[5], "pcols"], [T("kf")], bias=pcols[R, l * NCOL + 5:l * NCOL + 6])
                    TT("dve", tA[R, :], tA[R, :], cosT[R, :], ALU.mult, [T("ang"), T("cosT")], [T("ang")])
                    TT("dve", tB[R, :], tB[R, :], sinT[R, :], ALU.mult, [T("kf"), T("sinT")], [T("kf")])
                    TT("dve", krb[R, :], tA[R, :], tB[R, :], ALU.add, [T("ang"), T("kf")], [T("posi")])
                    for h in range(NH):
                        CP("dve" if h % 2 == 0 else "act", KT[R, h, t0:t0 + BT], krb[R, :], [T("posi")], [T("KT", h, b)])
                    for (j0, bk, dst) in ((0, 6, rq), (2, 7, rkv)):
                        for j in range(2):
                            MM(ps[bk][:, :], ones_r, csq[:, j0 + j, :], j == 0, j == 1, [T("csq", j0 + j), "ones_r"], PST[bk])
                        ACT(dst, ps[bk][:, :], AF.Ln, [PST[bk], T("epsr")], [T("r", j0)], bias=epsr)
                        ACT(dst, dst, AF.Exp, [T("r", j0)], [T("r", j0)], scale=-0.5)
                    for tt in range(4):
                        for j in range(2):
                            MM(ps[6][:, tt:tt + 1], csq[:, 2 + j, tt * 128:(tt + 1) * 128], ones_r[:, 0:1], j == 0, j == 1,
                               [T("csq", 2 + j), "ones_r", T("r", 0)], PST[6])
                    ACT(rtok[:, 0:4], ps[6][:, 0:4], AF.Ln, [PST[6], T("epsr")], [T("rtok")], bias=epsr)
                    ACT(rtok[:, 0:4], rtok[:, 0:4], AF.Exp, [T("rtok")], [T("rtok")], scale=-0.5)
                    TT("dve", srt[R, :], rq[R, :], sinT[R, :], ALU.mult, [T("r", 0), T("sinT")], [T("srt")])
                    TT("dve", rq[R, :], rq[R, :], cosT[R, :], ALU.mult, [T("r", 0), T("cosT"), T("srt")], [T("r", 0)])
                    for h in range(NH):
                        ba, bb = 2 * (h % 2), 2 * (h % 2) + 1
                        for k in range(2):
                            MM(ps[ba][0:96, :], wqa[:, k, h * 96:(h + 1) * 96], cg[:, k, :], k == 0, k == 1,
                               [T("wqa"), T("cg", k)], PST[ba])
                        for k in range(2):
                            MM(ps[bb][0:96, :], wqb[:, k, h * 96:(h + 1) * 96], cg[:, k, :], k == 0, k == 1,
                               [T("wqb"), T("cg", k)], PST[bb])
                        tq, tqt = (tA, T("ang")) if h % 2 == 0 else (tB, T("kf"))
                        TT("dve", qT[0:96, h, :], ps[ba][0:96, :], rq[0:96, :], ALU.mult, [PST[ba], T("r", 0)], [T("q", h)])
                        TT("dve", tq[R, :], ps[bb][R, :], srt[R, :], ALU.mult, [PST[bb], T("srt")], [tqt])
                        TT("dve", qT[R, h, :], qT[R, h, :], tq[R, :], ALU.add, [T("q", h), tqt], [T("q", h)])
                    for h in range(NH):
                        bk = 4 + (h % 2)
                        for k in range(2):
                            MM(ps[bk][0:64, :], wkv4[:, k, h, 0:64], cg[:, 2 + k, :], k == 0, k == 1,
                               [T("wkv"), T("cg", 2 + k)], PST[bk])
                        TT("dve", KT[0:64, h, t0:t0 + BT], ps[bk][0:64, :], rkv[0:64, :], ALU.mult, [PST[bk], T("r", 2)],
                           [T("KT", h, b)])
                    for tt in range(4):
                        bk = 6 + (tt % 2)
                        for k in range(2):
                            MM(ps[bk][:, :].rearrange("p (h d) -> p h d", h=8), cg[:, 2 + k, tt * 128:(tt + 1) * 128],
                               wkv4[:, k, :, 64:128], k == 0, k == 1, [T("wkv"), T("cg", 2 + k), T("rtok")], PST[bk])
                        ACT(VV[:, 4 * b + tt, :], ps[bk][:, :], AF.Identity, [PST[bk], T("rtok")], [T("V", 4 * b + tt)],
                            scale=rtok[:, tt:tt + 1])
                    pairs = []
                    for h in range(NH):
                        for kt in range(4 * b + 4):
                            pairs.append((h, kt))
                    nkt = 4 * b + 4

                    def score(i):
                        h, kt = pairs[i]
                        c0 = max(0, kt - 4 * b) * 128
                        bk = i % 3
                        MM(ps[bk][:, c0:BT], KT[0:96, h, kt * 128:(kt + 1) * 128], qT[0:96, h, c0:BT], True, True,
                           [T("KT", h, kt // 4), T("q", h)], PST[bk])
                        e = exps[i % 3]
                        ACT(e[:, c0:BT], ps[bk][:, c0:BT], AF.Exp, [PST[bk]], [T("csq", i % 3)], scale=sc)
                        if kt >= 4 * b:
                            MS("pool", e[64:128, c0:c0 + 64], 0.0, [], [T("csq", i % 3)])

                    def pv(i):
                        h, kt = pairs[i]
                        c0 = max(0, kt - 4 * b) * 128
                        e = exps[i % 3]
                        bo, bs = 3 + (h % 2), 5 + (h % 2)
                        if h % 2 == 0:
                            MM(ps[bo][0:64, c0:BT], VV[:, kt, h * 64:(h + 1) * 64], e[:, c0:BT], kt == 0, kt == nkt - 1,
                               [T("V", kt), T("csq", i % 3)], PST[bo])
                        else:
                            MM(ps[bo][:, c0:BT], VV[:, kt, (h - 1) * 64:(h + 1) * 64], e[:, c0:BT], kt == 0, kt == nkt - 1,
                               [T("V", kt), T("csq", i % 3)], PST[bo])
                        MM(ps[bs][:, c0:BT], ones_s, e[:, c0:BT], kt == 0, kt == nkt - 1, ["ones_s", T("csq", i % 3)], PST[bs])
                        if kt == nkt - 1:
                            rr = slice(0, 64) if h % 2 == 0 else slice(64, 128)
                            rc = rcp[h % 2]
                            ACT(rc[rr, :], ps[bs][rr, :], AF.Ln, [PST[bs]], [T("rc", h % 2)])
                            ACT(rc[rr, :], rc[rr, :], AF.Exp, [T("rc", h % 2)], [T("rc", h % 2)], scale=-1.0)
                            TT("dve", yatt[rr, h // 2, t0:t0 + BT], ps[bo][rr, :], rc[rr, :], ALU.mult,
                               [PST[bo], T("rc", h % 2)], [("yatt", b, h // 2)])

                    LA = 2
                    for i in range(len(pairs) + LA):
                        if i < len(pairs):
                            score(i)
                        if i >= LA:
                            pv(i - LA)
                if stop_after == (l, 1):
                    break

                T = newsweep()
                XBENG = XB23
                A.off = A.base
                _skip = A._take(8 * 544 // 2 + 8 * 96 // 2 + 768 + 768 + 1024 + 8 * SEQ // 2 + 16 * 512 // 2)
                yatt = A.bf(4 * SEQ).rearrange("p (c t) -> p c t", c=4)
                yatt_end = A.off
                A.off = A.base
                wu = A.bf(8 * 1024).rearrange("p (k n) -> p k n", k=8)
                wo = A.bf(8 * 1024).rearrange("p (k n) -> p k n", k=8)
                dg = [A.bf(31 * 128).rearrange("p (t n) -> p t n", t=31) for _ in range(4)]
                sg = [A.f32(512)] * 2
                assert A.off <= yatt_end - 4 * SEQ // 2, "sweep2 prefix overlaps yatt"
                A.off = yatt_end
                xb = A.bf(8 * 512).rearrange("p (c t) -> p c t", c=8)
                hh = A.bf(4 * 544).rearrange("p (c t) -> p c t", c=4)
                cv = A.f32(4 * 512).rearrange("p (c t) -> p c t", c=4)
                ycv = A.bf(4 * 512).rearrange("p (c t) -> p c t", c=4)
                lt = ln_tmp(8)
                MS("pool", lt["eps"], LN_EPS, [], [T("eps")])
                win = wbf["mix_w_in"][l].rearrange("(k p) n -> p k n", p=128)
                DMA("sp", wu, win[:, :, 544:1568], [("wbf", "mix_w_in", l)], [T("wu")], "wA")
                DMA("sp", wo, wbf["mix_w_o"][l].rearrange("(k p) n -> p k n", p=128), [("wbf", "mix_w_o", l)], [T("wo")], "wB")
                MS("pool", hh[:, :, 0:32], 0.0, [], [T("hh", c) for c in range(4)])
                deferred = []
                xbt = [T("xb", c) for c in range(8)]

                def stA1(b):
                    t0 = b * BT
                    for c in range(8):
                        CP(XBENG[c % 2], xb[:, c, :], resid[:, c, t0:t0 + BT], [RT(b, c)], [T("xb", c)])

                def stA2(b):
                    if b == 0:
                        for c in range(4):
                            for tp in range(31):
                                TS("pool" if c < 2 else "act", dg[c][:, tp, :], identb, pc(l, 78 + c * 31 + tp), None, ALU.mult, None,
                                   ["identb", "pcols"], [T("dg", c)])
                    if b > 0:
                        for c in range(4):
                            CP("pool", hh[:, c, 2:32], hh[:, c, 514:544], [T("hh", c)], [T("hh", c)])
                    for c in range(4):
                        bg_, ba_ = (c % 3) * 2, (c % 3) * 2 + 1
                        for k in range(8):
                            MM(ps[bg_][:, :], wu[:, k, 512 + c * 128:512 + (c + 1) * 128], xb[:, k, :], k == 0, k == 7,
                               [T("wu"), xbt[k]], PST[bg_])
                        for k in range(8):
                            MM(ps[ba_][:, :], wu[:, k, c * 128:(c + 1) * 128], xb[:, k, :], k == 0, k == 7,
                               [T("wu"), xbt[k]], PST[ba_])
                        ACT(sg[c % 2], ps[bg_][:, :], AF.Sigmoid, [PST[bg_], "pcols"], [T("sg")], bias=pc(l, 10 + c))
                        STT(hh[:, c, 32:544], ps[ba_][:, :], pc(l, 6 + c), sg[c % 2], ALU.add, ALU.mult,
                            [PST[ba_], T("sg"), "pcols"], [T("hh", c)])

                def stB(b):
                    for c in range(4):
                        d = dg[c]
                        bk = (4, 5, 0, 1)[c]
                        for tp in range(31):
                            MM(ps[bk][:, :], d[:, tp, :], hh[:, c, 2 + tp:2 + tp + BT], tp == 0, tp == 30,
                               [T("dg", c), T("hh", c)], PST[bk])
                        ACT(cv[:, c, :], ps[bk][:, :], AF.Identity, [PST[bk], "pcols"], [T("cv", c)], bias=pc(l, 18 + c))

                def stC(b, phase):
                    layer_norm(T, lambda c: cv[:, c, :], lambda c: [T("cv", c)], 4, ones_c, LN_EPS,
                               lambda c: pc(l, 22 + c), lambda c: pc(l, 26 + c), lambda c: ycv[:, c, :],
                               lambda c: [T("ycv", c)], lt, 6, 7, silu=True, phase=phase)

                def stD(b):
                    t0 = b * BT
                    for o in range(8):
                        bk = o % 4
                        for k in range(8):
                            rhs = yatt[:, k, t0:t0 + BT] if k < 4 else ycv[:, k - 4, :]
                            rt = ("yatt", b, k) if k < 4 else T("ycv", k - 4)
                            MM(ps[bk][:, :], wo[:, k, o * 128:(o + 1) * 128], rhs, k == 0, k == 7, [T("wo"), rt], PST[bk])
                        STT(resid[:, o, t0:t0 + BT], resid[:, o, t0:t0 + BT], ALPHA, ps[bk][:, :], ALU.mult, ALU.add,
                            [PST[bk], RT(b, o)], [RT(b, o)])

                def stL(b, phase):
                    t0 = b * BT
                    layer_norm(T, lambda c: resid[:, c, t0:t0 + BT], lambda c: [RT(b, c)], 8, ones_d, LN_EPS,
                               lambda c: pc(l, 30 + c), lambda c: pc(l, 38 + c), lambda c: resid[:, c, t0:t0 + BT],
                               lambda c: [RT(b, c)], lt, 6, 7, phase=phase, pool_chain=(b < NBLK - 1))

                stA1(0)
                stA2(0)
                stA1(1)
                stB(0)
                stC(0, 1)
                for b in range(NBLK):
                    nxt = b + 1 < NBLK
                    stC(b, 2)
                    if nxt:
                        stA2(b + 1)
                    stD(b)
                    stL(b, 1)
                    if b + 2 < NBLK:
                        stA1(b + 2)
                    stL(b, 2)
                    if nxt:
                        stB(b + 1)
                        stC(b + 1, 1)
                if stop_after == (l, 2):
                    break

                T = newsweep()
                wq = A.bf(8 * 1024).rearrange("p (k n) -> p k n", k=8)
                wo = A.bf(8 * 1024).rearrange("p (k n) -> p k n", k=8)
                kx = A.bf(8 * MEM).rearrange("p (c t) -> p c t", c=8)
                vx = A.bf(2 * 1024).rearrange("p (m n) -> p m n", m=2)
                xb = A.bf(8 * 512).rearrange("p (c t) -> p c t", c=8)
                qx = A.bf(8 * 512).rearrange("p (c t) -> p c t", c=8)
                yx = A.bf(8 * 512).rearrange("p (c t) -> p c t", c=8)
                exps = [A.bf(512) for _ in range(4)]
                rcp = [A.f32(512) for _ in range(2)]
                lt = ln_tmp(8)
                wkvx2d = A.bf(8 * 2048)
                wkvx = wkvx2d.rearrange("p (k n) -> p k n", k=8)
                qxs = [qx, wkvx2d[:, 0:4096].rearrange("p (c t) -> p c t", c=8)]
                yxs = [yx, wkvx2d[:, 4096:8192].rearrange("p (c t) -> p c t", c=8)]
                MS("pool", lt["eps"], LN_EPS, [], [T("eps")])
                if s == 0 and RL_ENG != "none":
                    srci = wbf["ffn_w_in"][l].rearrange("(k p) n -> p k n", p=128)
                    for g in range(NHC // 2):
                        dsti = wrl["ffn_w_in"][l, g].rearrange("p (k n) -> p k n", k=8)
                        DMA(RL_ENG, dsti[:, :, 0:256], srci[:, :, g * 256:(g + 1) * 256], [("wbf", "ffn_w_in", l)],
                            [("wrl", "ffn_w_in", l)], f"rl_{l}", grouped=True)
                        DMA(RL_ENG, dsti[:, :, 256:512], srci[:, :, FFN_H + g * 256:FFN_H + (g + 1) * 256], [("wbf", "ffn_w_in", l)],
                            [("wrl", "ffn_w_in", l)], f"rl_{l}", grouped=True)
                    srcd = wbf["ffn_w_down"][l].rearrange("(j p) (o n) -> o p j n", p=128, n=128)
                    for o in range(8):
                        dstd = wrl["ffn_w_down"][l, o].rearrange("p (j n) -> p j n", j=NHC)
                        DMA(RL_ENG, dstd, srcd[o], [("wbf", "ffn_w_down", l)], [("wrl", "ffn_w_down", l)], f"rl_{l}", grouped=True, nonc=False)
                DMA("sp", wkvx, wbf["xa_w_kv"][l].rearrange("(k p) n -> p k n", p=128), [("wbf", "xa_w_kv", l)], [T("wkvx")], "wA")
                DMA("sp", wq, wbf["xa_w_q"][l].rearrange("(k p) n -> p k n", p=128), [("wbf", "xa_w_q", l)], [T("wq")], "wB")
                DMA("sp", wo, wbf["xa_w_o"][l].rearrange("(k p) n -> p k n", p=128), [("wbf", "xa_w_o", l)], [T("wo")], "wC")
                for oc in range(8):
                    bk = oc % 4
                    for k in range(8):
                        MM(ps[bk][:, 0:MEM], wkvx[:, k, oc * 128:(oc + 1) * 128], memT[:, k, :], k == 0, k == 7,
                           [T("wkvx"), "memT"], PST[bk])
                    CP("act", kx[:, oc, :], ps[bk][:, 0:MEM], [PST[bk]], [T("kx")])
                for mt in range(2):
                    for hv in range(2):
                        bk = 4 + (2 * mt + hv)
                        for k in range(8):
                            MM(ps[bk][:, :], memT[:, k, mt * 128:(mt + 1) * 128], wkvx[:, k, 1024 + hv * 512:1024 + (hv + 1) * 512],
                               k == 0, k == 7, [T("wkvx"), "memT"], PST[bk])
                        CP("dve", vx[:, mt, hv * 512:(hv + 1) * 512], ps[bk][:, :], [PST[bk]], [T("vx")])
                xbt = [T("xb", c) for c in range(8)]

                def xA1(b):
                    t0 = b * BT
                    for c in range(8):
                        CP(XBENG[c % 2], xb[:, c, :], resid[:, c, t0:t0 + BT], [RT(b, c)], [T("xb", c)])

                def xA2(b):
                    q_ = qxs[b % 2]
                    for o in range(8):
                        bk = 6 + o % 2
                        for k in range(8):
                            MM(ps[bk][:, :], wq[:, k, o * 128:(o + 1) * 128], xb[:, k, :], k == 0, k == 7, [T("wq"), xbt[k]], PST[bk])
                        CP("act" if o % 2 == 0 else "dve", q_[:, o, :], ps[bk][:, :], [PST[bk]], [T("qx", b % 2, o), T("wkvx")])

                def xs(b, h):
                    q_ = qxs[b % 2]
                    for mt in range(2):
                        bk = (h % 2) * 2 + mt
                        for dc in range(2):
                            MM(ps[bk][:, :], kx[:, 2 * h + dc, mt * 128:(mt + 1) * 128], q_[:, 2 * h + dc, :], dc == 0, dc == 1,
                               [T("kx"), T("qx", b % 2, 2 * h + dc)], PST[bk])
                        ei = (2 * h + mt) % 4
                        ACT(exps[ei], ps[bk][:, :], AF.Exp, [PST[bk]], [T("e", ei)], scale=1.0 / 16.0)

                def xpv(b, h):
                    y_ = yxs[b % 2]
                    for dvc in range(2):
                        bk = 4 + dvc
                        for mt in range(2):
                            ei = (2 * h + mt) % 4
                            MM(ps[bk][:, :], vx[:, mt, (2 * h + dvc) * 128:(2 * h + dvc + 1) * 128], exps[ei], mt == 0, mt == 1,
                               [T("vx"), T("e", ei)], PST[bk])
                    bs = 6 + (h % 2)
                    for mt in range(2):
                        ei = (2 * h + mt) % 4
                        MM(ps[bs][:, :], ones_s, exps[ei], mt == 0, mt == 1, ["ones_s", T("e", ei)], PST[bs])
                    rc = rcp[h % 2]
                    ACT(rc, ps[bs][:, :], AF.Ln, [PST[bs]], [T("rc", h % 2)])
                    ACT(rc, rc, AF.Exp, [T("rc", h % 2)], [T("rc", h % 2)], scale=-1.0)
                    for dvc in range(2):
                        TT("dve", y_[:, 2 * h + dvc, :], ps[4 + dvc][:, :], rc, ALU.mult, [PST[4 + dvc], T("rc", h % 2)],
                           [T("yx", b % 2, 2 * h + dvc), T("wkvx")])

                def xD(b):
                    t0 = b * BT
                    y_ = yxs[b % 2]
                    for o in range(8):
                        bk = o % 2
                        for k in range(8):
                            MM(ps[bk][:, :], wo[:, k, o * 128:(o + 1) * 128], y_[:, k, :], k == 0, k == 7,
                               [T("wo"), T("yx", b % 2, k)], PST[bk])
                        STT(resid[:, o, t0:t0 + BT], resid[:, o, t0:t0 + BT], ALPHA, ps[bk][:, :], ALU.mult, ALU.add,
                            [PST[bk], RT(b, o)], [RT(b, o)])

                def xL(b, phase):
                    t0 = b * BT
                    layer_norm(T, lambda c: resid[:, c, t0:t0 + BT], lambda c: [RT(b, c)], 8, ones_d, LN_EPS,
                               lambda c: pc(l, 46 + c), lambda c: pc(l, 54 + c), lambda c: resid[:, c, t0:t0 + BT],
                               lambda c: [RT(b, c)], lt, 2, 3, phase=phase, pool_chain=(b < NBLK - 1))

                xA1(0)
                xA2(0)
                xA1(1)
                for b in range(NBLK):
                    xs(b, 0)
                    xs(b, 1)
                    if b + 1 < NBLK:
                        xA2(b + 1)
                    if b > 0:
                        xL(b - 1, 2)
                    xpv(b, 0)
                    xs(b, 2)
                    xpv(b, 1)
                    xs(b, 3)
                    xpv(b, 2)
                    xpv(b, 3)
                    xD(b)
                    xL(b, 1)
                    if b + 2 < NBLK:
                        xA1(b + 2)
                xL(NBLK - 1, 2)
                if stop_after == (l, 3):
                    break

                T = newsweep()
                xb2 = A.bf(8 * 1024).rearrange("p (c t) -> p c t", c=8)
                hT = A.bf(NHC * 1024).rearrange("p (c t) -> p c t", c=NHC)
                wring2 = [A.bf(8 * 512) for _ in range(3)]
                dring2 = [A.bf(NHC * 128) for _ in range(2)]
                wring = [w.rearrange("p (k n) -> p k n", k=8) for w in wring2]
                dring = [w.rearrange("p (k n) -> p k n", k=NHC) for w in dring2]
                sl = [A.f32(512) for _ in range(3)]
                lt = ln_tmp(8)
                ots = [A.f32(1024), A.f32(1024)]
                MS("pool", lt["eps"], LN_EPS, [], [T("eps")])
                wfi = wbf["ffn_w_in"][l].rearrange("(k p) n -> p k n", p=128)
                wfd = wbf["ffn_w_down"][l].rearrange("(k p) n -> p k n", p=128)
                last = (l == depth - 1)
                n_sl = 0
                def xb2_cast(hf):
                    for c in range(8):
                        CP("act" if c % 2 == 0 else "dve", xb2[:, c, :], resid[:, c, hf * 1024:(hf + 1) * 1024],
                           [RT(2 * hf, c), RT(2 * hf + 1, c)], [T("xb2", c)])

                def ln3(phase, b):
                    t0 = b * BT
                    layer_norm(T, lambda c: resid[:, c, t0:t0 + BT], lambda c: [RT(b, c)], 8, ones_d, LN_EPS,
                               lambda c: pc(l, 62 + c), lambda c: pc(l, 70 + c), lambda c: resid[:, c, t0:t0 + BT],
                               lambda c: [RT(b, c)], lt, 6, 7, phase=phase, pool_chain=(b < 2))

                def emit_out(b, banks=(4, 5)):
                    t0 = b * BT
                    for tt in range(4):
                        oi = n_ot[0] % 2
                        ot = ots[oi]
                        n_ot[0] += 1
                        for half in range(2):
                            bk = banks[(2 * tt + half) % len(banks)]
                            for j in range(4):
                                c = half * 4 + j
                                TR(ps[bk][:, j * 128:(j + 1) * 128], resid[:, c, t0 + tt * 128:t0 + (tt + 1) * 128], ident_f,
                                   [RT(b, c), "ident"], PST[bk])
                            CP("dve" if half == 0 else "act", ot[:, half * 512:(half + 1) * 512], ps[bk][:, :], [PST[bk]],
                               [T("ot", oi, half)])
                        outtoks.append(("outdram", len(outtoks)))
                        DMA("sp", out_d[s, t0 + tt * 128:t0 + (tt + 1) * 128, :], ot, [T("ot", oi, 0), T("ot", oi, 1)],
                            [outtoks[-1]], f"ot{oi}")

                def finish(b):
                    ln3(2, b)
                    if last:
                        emit_out(b, banks=(0, 1, 2, 3, 4, 5))

                n_ot = [0]
                xbt = [T("xb2", c) for c in range(8)]
                pending = []
                xb2_cast(0)
                for hf in range(2):
                    for g in range(NHC // 2):
                        ri = (hf * (NHC // 2) + g) % 3
                        wr = wring[ri]
                        DMA("sp", wr[:, :, 0:256], wfi[:, :, g * 256:(g + 1) * 256], [("wbf", "ffn_w_in", l)], [T("wr", ri)], f"wr{ri}a")
                        DMA("sp", wr[:, :, 256:512], wfi[:, :, FFN_H + g * 256:FFN_H + (g + 1) * 256], [("wbf", "ffn_w_in", l)],
                            [T("wr", ri)], f"wr{ri}b")
                        for bb in range(2):
                            for cc in range(2):
                                j = 2 * g + cc
                                bg_, bu_ = 2 * ((2 * bb + cc) % 2), 2 * ((2 * bb + cc) % 2) + 1
                                for k in range(8):
                                    MM(ps[bg_][:, :], wr[:, k, cc * 128:(cc + 1) * 128], xb2[:, k, bb * 512:(bb + 1) * 512],
                                       k == 0, k == 7, [T("wr", ri), xbt[k]], PST[bg_])
                                for k in range(8):
                                    MM(ps[bu_][:, :], wr[:, k, 256 + cc * 128:256 + (cc + 1) * 128], xb2[:, k, bb * 512:(bb + 1) * 512],
                                       k == 0, k == 7, [T("wr", ri), xbt[k]], PST[bu_])
                                si = n_sl % 3
                                n_sl += 1
                                ACT(sl[si], ps[bg_][:, :], AF.Silu, [PST[bg_]], [T("sl", si)])
                                TT("dve", hT[:, j, bb * 512:(bb + 1) * 512], ps[bu_][:, :], sl[si], ALU.mult,
                                   [PST[bu_], T("sl", si)], [T("hT", j, bb)])
                        if g in (1, 3, 5) and pending:
                            pending.pop(0)()
                    while pending:
                        pending.pop(0)()
                    if hf == 0:
                        xb2_cast(1)
                    for o in range(8):
                        di = (hf * 8 + o) % 2
                        dr = dring[di]
                        DMA("sp", dr, wfd[:, :, o * 128:(o + 1) * 128], [("wbf", "ffn_w_down", l)], [T("dr", di)], f"dr{di}")
                        for bb in range(2):
                            b = 2 * hf + bb
                            bk = 4 + (2 * o + bb) % 4
                            for j in range(NHC):
                                MM(ps[bk][:, :], dr[:, j, :], hT[:, j, bb * 512:(bb + 1) * 512], j == 0, j == NHC - 1,
                                   [T("dr", di), T("hT", j, bb)], PST[bk])
                            STT(resid[:, o, b * BT:(b + 1) * BT], resid[:, o, b * BT:(b + 1) * BT], ALPHA, ps[bk][:, :],
                                ALU.mult, ALU.add, [PST[bk], RT(b, o)], [RT(b, o)])
                    b0, b1 = 2 * hf, 2 * hf + 1
                    ln3(1, b0)
                    if hf == 0:
                        def item0(b0=b0, b1=b1):
                            ln3(2, b0)
                            ln3(1, b1)

                        def item1(b0=b0, b1=b1):
                            if last:
                                emit_out(b0)
                            ln3(2, b1)

                        def item2(b1=b1):
                            if last:
                                emit_out(b1)
                        pending.append(item0)
                        pending.append(item1)
                        pending.append(item2)
                    else:
                        ln3(2, b0)
                        ln3(1, b1)
                        ln3(2, b1)
                        if last:
                            emit_out(b0, banks=(0, 1, 2, 3, 4, 5))
                            emit_out(b1, banks=(0, 1, 2, 3, 4, 5))
                if stop_after == (l, 4):
                    break
            if stop_after is not None:
                break
        if stop_after is not None:
            P.barrier()
            if stop_after[1] == 1:
                for c in range(4):
                    CP("dve", resid[:, c, :], yatt[:, c, :], [("yatt", bb, c) for bb in range(4)], [RT(bb, c) for bb in range(4)])
            dv = out_d[0].rearrange("(a b) d -> a (b d)", a=1024)
            for c in range(8):
                outtoks.append(("outdram", len(outtoks)))
                DMA("sp", dv[c * 128:(c + 1) * 128, :], resid[:, c, :], [RT(bb, c) for bb in range(4)], [outtoks[-1]], "dbg")
        P.barrier()
        P.op("sp", lambda e: e.nop(), list(outtoks), [])
        P.emit(st)
        build.info = dict(n_ops=len(P.ops), n_sems=P.n_sems, peak=A.peak)
    return nc


def _host_layout(inp):
    L = DEPTH
    pcols = np.zeros((128, L * NCOL), np.float32)

    def put(l, j, vec):
        v = np.asarray(vec, np.float32).reshape(-1, 128)
        for i in range(v.shape[0]):
            pcols[:, l * NCOL + j + i] = v[i]
    for l in range(L):
        b_in = np.asarray(inp["mix_b_in"][l], np.float32)
        put(l, 0, b_in[0:512])
        pcols[64:96, l * NCOL + 4] = b_in[512:544]
        pcols[64:80, l * NCOL + 5] = b_in[528:544]
        pcols[80:96, l * NCOL + 5] = b_in[512:528]
        put(l, 6, b_in[544:1056])
        put(l, 10, b_in[1056:1568])
        put(l, 14, inp["mla_q_norm"][l])
        put(l, 16, inp["mla_kv_norm"][l])
        put(l, 18, inp["conv_dw_b"][l])
        put(l, 22, inp["conv_norm_g"][l])
        put(l, 26, inp["conv_norm_b"][l])
        put(l, 30, inp["ln1_g"][l]); put(l, 38, inp["ln1_b"][l])
        put(l, 46, inp["ln2_g"][l]); put(l, 54, inp["ln2_b"][l])
        put(l, 62, inp["ln3_g"][l]); put(l, 70, inp["ln3_b"][l])
        dw = np.asarray(inp["conv_dw_w"][l], np.float32)
        for c in range(4):
            pcols[:, l * NCOL + 78 + c * 31:l * NCOL + 78 + (c + 1) * 31] = dw[:, c * 128:(c + 1) * 128].T
    cvec = np.zeros((128, 4), np.float32)
    inv_freq = (np.float32(10000.0) ** (-np.arange(0, 32, 2, dtype=np.float32) / np.float32(32))).astype(np.float32)
    for p in range(128):
        cvec[p, 0] = inv_freq[p % 16]
        cvec[p, 1] = -1.0 if (p % 32) < 16 else 1.0
    return pcols, cvec


_NC_CACHE = {}


def kernel(**inputs):
    nseq = 32 // N_CORES
    if "nc" not in _NC_CACHE:
        _NC_CACHE["nc"] = build(nseq=nseq, depth=DEPTH)
    nc = _NC_CACHE["nc"]
    pcols, cvec = _host_layout(inputs)
    ident = np.eye(128, dtype=np.float32)
    x = np.ascontiguousarray(np.asarray(inputs["x"], np.float32))
    mem = np.ascontiguousarray(np.asarray(inputs["mem"], np.float32))
    pos = np.ascontiguousarray(np.asarray(inputs["positions"], np.int32))
    shared = {n: np.ascontiguousarray(np.asarray(inputs[n], np.float32)) for n in WNAMES}
    in_maps = []
    for i in range(N_CORES):
        m = dict(shared)
        m["x"] = x[i * nseq:(i + 1) * nseq]
        m["mem"] = mem[i * nseq:(i + 1) * nseq]
        m["positions"] = pos[i * nseq:(i + 1) * nseq]
        m["pcols"] = pcols
        m["cvec"] = cvec
        m["ident"] = ident
        in_maps.append(m)
    res = run_bass_kernel_spmd(nc, in_maps, core_ids=list(range(N_CORES)))
    return np.concatenate([np.asarray(r["out"], np.float32) for r in res.results], axis=0)
```

```python
import contextlib
import math
import numpy as np
import concourse.bass as bass
import concourse.mybir as mybir
from concourse.bass_utils import run_bass_kernel_spmd

F32 = mybir.dt.float32
BF16 = mybir.dt.bfloat16
I32 = mybir.dt.int32
AF = mybir.ActivationFunctionType
ALU = mybir.AluOpType

D = 1024
SEQ = 2048
MEM = 256
DEPTH = 2
BT = 512
NBLK = SEQ // BT
NH = 8
IN_COLS = 1568
FFN_H = 2816
NHC = FFN_H // 128
ALPHA = (2.0 * DEPTH) ** 0.25
LN_EPS = 1e-5
RMS_EPS = 1e-6
NCOL = 208
N_CORES = 8
RL_ENG = "none"
DG_ENG = "act"
XB23 = ("act", "dve")
SEM_CAP = 16000
TWO_PI = 2.0 * math.pi
CW1 = 6.28125
CW2 = TWO_PI - CW1
PI_SAFE = 3.14159

WNAMES = ["mix_w_in", "mla_w_uq", "mla_w_ukv", "mix_w_o", "xa_w_kv", "xa_w_q", "xa_w_o",
          "ffn_w_in", "ffn_w_down"]
WSHAPES = {"mix_w_in": (1024, 1568), "mla_w_uq": (256, 768), "mla_w_ukv": (256, 1024),
           "mix_w_o": (1024, 1024), "xa_w_kv": (1024, 2048), "xa_w_q": (1024, 1024),
           "xa_w_o": (1024, 1024), "ffn_w_in": (1024, 5632), "ffn_w_down": (2816, 1024)}


class Prog:
    ENGS = ("pe", "act", "dve", "pool", "sp")

    def __init__(self, nc):
        self.nc = nc
        self.ops = []
        self.last_w = {}
        self.readers = {}
        self.dma_slots = {}
        self.eng_ops = {e: [] for e in self.ENGS}
        self.bar = {e: set() for e in self.ENGS}

    def op(self, eng, fn, reads=(), writes=(), dma_slot=None, grouped=False):
        i = len(self.ops)
        deps = set(self.bar[eng])
        self.bar[eng] = set()
        for t in reads:
            w = self.last_w.get(t)
            if w is not None:
                deps.add(w)
        for t in writes:
            w = self.last_w.get(t)
            if w is not None:
                deps.add(w)
            deps.update(self.readers.get(t, ()))
        rec = dict(id=i, eng=eng, fn=fn, deps=deps, dma=dma_slot, ms=None, dmaval=None)
        if dma_slot is not None:
            st = self.dma_slots.setdefault(dma_slot, dict(count=0, last=None, grouped=grouped))
            if st["last"] is not None and not grouped:
                deps.add(st["last"])
            st["count"] += 1
            st["last"] = i
            rec["dmaval"] = st["count"]
        deps.discard(i)
        self.ops.append(rec)
        self.eng_ops[eng].append(rec)
        for t in reads:
            self.readers.setdefault(t, []).append(i)
        for t in writes:
            self.last_w[t] = i
            self.readers[t] = []
        return i

    def barrier(self):
        last = set()
        for e in self.ENGS:
            if self.eng_ops[e]:
                last.add(self.eng_ops[e][-1]["id"])
        for st in self.dma_slots.values():
            if st["last"] is not None:
                last.add(st["last"])
        for e in self.ENGS:
            self.bar[e] |= last

    def emit(self, stack):
        nc = self.nc
        ops = self.ops
        needed = set()
        for r in ops:
            for d in r["deps"]:
                p = ops[d]
                if p["dma"] is not None:
                    continue
                if p["eng"] == "pe" and r["eng"] == "pe" and r["dma"] is None:
                    continue
                needed.add(d)
        cnt = {e: 0 for e in self.ENGS}
        for r in ops:
            if r["id"] in needed:
                cnt[r["eng"]] += 1
                r["ms"] = cnt[r["eng"]]
        esems = {}
        for e in self.ENGS:
            n = max(1, (cnt[e] + SEM_CAP - 1) // SEM_CAP)
            esems[e] = [stack.enter_context(nc.semaphore(f"s_{e}{k}")) for k in range(n)]
        per = SEM_CAP // 16
        dsems = {}
        for s, st in self.dma_slots.items():
            n = max(1, (st["count"] + per - 1) // per)
            j = len(dsems)
            dsems[s] = [stack.enter_context(nc.semaphore(f"d{j}_{k}")) for k in range(n)]
        self.n_sems = sum(len(v) for v in esems.values()) + sum(len(v) for v in dsems.values())

        def run_engine(ename):
            def body(eng):
                waited = {}
                for r in self.eng_ops[ename]:
                    reqs = {}
                    for d in r["deps"]:
                        p = ops[d]
                        if p["dma"] is not None:
                            dv = self.dma_slots[p["dma"]]["count"] if self.dma_slots[p["dma"]]["grouped"] else p["dmaval"]
                            k = (dv - 1) // per
                            v = ((dv - 1) % per + 1) * 16
                            key = ("d", p["dma"], k)
                            sem = dsems[p["dma"]][k]
                        else:
                            if p["eng"] == "pe" and ename == "pe" and r["dma"] is None:
                                continue
                            m = p["ms"]
                            k = (m - 1) // SEM_CAP
                            v = (m - 1) % SEM_CAP + 1
                            key = ("e", p["eng"], k)
                            sem = esems[p["eng"]][k]
                        if v > reqs.get(key, (0, None))[0]:
                            reqs[key] = (v, sem)
                    for key, (v, sem) in reqs.items():
                        if waited.get(key, 0) >= v:
                            continue
                        waited[key] = v
                        eng.wait_ge(sem, v)
                    ins = r["fn"](eng)
                    if r["dma"] is not None:
                        k = (r["dmaval"] - 1) // per
                        ins.then_inc(dsems[r["dma"]][k], 16)
                    elif r["ms"] is not None:
                        k = (r["ms"] - 1) // SEM_CAP
                        ins.then_inc(esems[ename][k], 1)
            return body

        with nc.Block() as block:
            block.tensor(run_engine("pe"))
            block.scalar(run_engine("act"))
            block.vector(run_engine("dve"))
            block.gpsimd(run_engine("pool"))
            block.sync(run_engine("sp"))


class Arena:
    def __init__(self, t, total):
        self.t, self.total, self.off, self.base = t, total, 0, 0
        self.peak = 0

    def _take(self, n):
        a = self.off
        self.off += n
        self.peak = max(self.peak, self.off)
        assert self.off <= self.total, f"SBUF arena overflow {self.off} > {self.total}"
        return self.t[:, a:a + n]

    def f32(self, n):
        return self._take(n)

    def bf(self, n):
        assert n % 2 == 0
        return self._take(n // 2).bitcast(BF16)

    def i32(self, n):
        return self._take(n).bitcast(I32)

    def mark(self):
        self.base = self.off

    def reset(self):
        self.off = self.base


def build(nseq=4, depth=DEPTH, stop_after=None):
    nc = bass.Bass("TRN2", target_bir_lowering=False)
    x_d = nc.dram_tensor("x", [nseq, SEQ, D], F32, kind="ExternalInput").ap()
    mem_d = nc.dram_tensor("mem", [nseq, MEM, D], F32, kind="ExternalInput").ap()
    pos_d = nc.dram_tensor("positions", [nseq, SEQ], I32, kind="ExternalInput").ap()
    pcols_d = nc.dram_tensor("pcols", [128, DEPTH * NCOL], F32, kind="ExternalInput").ap()
    cvec_d = nc.dram_tensor("cvec", [128, 4], F32, kind="ExternalInput").ap()
    ident_d = nc.dram_tensor("ident", [128, 128], F32, kind="ExternalInput").ap()
    w32, wbf, wrl = {}, {}, {}
    for n in WNAMES:
        r, c = WSHAPES[n]
        w32[n] = nc.dram_tensor(n, [DEPTH, r, c], F32, kind="ExternalInput").ap()
        wbf[n] = nc.dram_tensor(n + "_bf", [DEPTH, r, c], BF16, kind="Internal").ap()
        if n == "ffn_w_in":
            wrl[n] = nc.dram_tensor(n + "_rl", [DEPTH, NHC // 2, 128, 8 * 512], BF16, kind="Internal").ap()
        elif n == "ffn_w_down":
            wrl[n] = nc.dram_tensor(n + "_rl", [DEPTH, 8, 128, NHC * 128], BF16, kind="Internal").ap()
    out_d = nc.dram_tensor("out", [nseq, SEQ, D], F32, kind="ExternalOutput").ap()

    with contextlib.ExitStack() as st:
        TOTAL = 53100
        arena_t = st.enter_context(nc.sbuf_tensor("arena", [128, TOTAL], F32))
        ps = [st.enter_context(nc.psum_tensor(f"ps{i}", [128, 512], F32)) for i in range(8)]
        PST = [f"ps{i}" for i in range(8)]
        A = Arena(arena_t, TOTAL)
        P = Prog(nc)

        def MM(out, lhsT, rhs, start, stop, reads, wtok):
            P.op("pe", lambda e: e.matmul(out, lhsT=lhsT, rhs=rhs, start=start, stop=stop), reads, [wtok])

        def TR(out, in_, ident, reads, wtok):
            P.op("pe", lambda e: e.transpose(out=out, in_=in_, identity=ident), reads, [wtok])

        def ACT(out, in_, func, reads, writes, scale=None, bias=None):
            kw = {}
            if scale is not None:
                kw["scale"] = scale
            if bias is not None:
                kw["bias"] = bias
            P.op("act", lambda e: e.activation(out=out, in_=in_, func=func, **kw), reads, writes)

        def TT(eng, out, in0, in1, op, reads, writes):
            P.op(eng, lambda e: e.tensor_tensor(out=out, in0=in0, in1=in1, op=op), reads, writes)

        def TS(eng, out, in0, s1, s2, op0, op1, reads, writes):
            if eng == "act":
                assert op1 is None and op0 == ALU.mult
                P.op("act", lambda e: e.activation(out=out, in_=in0, func=AF.Copy, scale=s1), reads, writes)
            elif op1 is None and eng == "pool" and op0 == ALU.mult:
                P.op(eng, lambda e: e.tensor_scalar(out=out, in0=in0, scalar1=s1, scalar2=0.0, op0=ALU.mult, op1=ALU.add), reads, writes)
            elif op1 is None:
                P.op(eng, lambda e: e.tensor_scalar(out=out, in0=in0, scalar1=s1, scalar2=None, op0=op0), reads, writes)
            else:
                P.op(eng, lambda e: e.tensor_scalar(out=out, in0=in0, scalar1=s1, scalar2=s2, op0=op0, op1=op1), reads, writes)

        def STT(out, in0, scalar, in1, op0, op1, reads, writes):
            P.op("dve", lambda e: e.scalar_tensor_tensor(out=out, in0=in0, scalar=scalar, in1=in1, op0=op0, op1=op1), reads, writes)

        def CP(eng, out, in_, reads, writes):
            if eng == "act":
                P.op("act", lambda e: e.activation(out=out, in_=in_, func=AF.Copy), reads, writes)
            else:
                P.op(eng, lambda e: e.tensor_copy(out=out, in_=in_), reads, writes)

        def MS(eng, ap, val, reads, writes):
            P.op(eng, lambda e: e.memset(ap, val), reads, writes)

        def DMA(eng, out, in_, reads, writes, slot, nonc=False, grouped=False):
            if grouped:
                P.op(eng, lambda e: e.dma_start(out=out, in_=in_), reads, writes, dma_slot=slot, grouped=True)
            elif nonc:
                def f(e):
                    with nc.allow_non_contiguous_dma(reason="small strided layout load"):
                        return e.dma_start(out=out, in_=in_)
                P.op(eng, f, reads, writes, dma_slot=slot)
            else:
                P.op(eng, lambda e: e.dma_start(out=out, in_=in_), reads, writes, dma_slot=slot)

        resid = A.f32(8 * SEQ).rearrange("p (c t) -> p c t", c=8)
        ident_f = A.f32(128)
        ones_s = A.bf(128)
        ones_d = A.bf(128)
        ones_c = A.bf(128)
        ones_r = A.bf(128)
        identb = A.bf(128)
        cvec = A.f32(4)
        pcols = A.f32(DEPTH * NCOL)
        memT = A.bf(8 * MEM).rearrange("p (c t) -> p c t", c=8)
        A.mark()

        def RT(b, c):
            return ("resid", b, c)

        def RTA(b):
            return [("resid", b, c) for c in range(8)]

        DMA("sp", ident_f, ident_d, [], ["ident"], "c0")
        DMA("sp", cvec, cvec_d, [], ["cvec"], "c1")
        DMA("sp", pcols, pcols_d, [], ["pcols"], "c2")
        for l in range(depth):
            for n in WNAMES:
                r, c = WSHAPES[n]
                k = c if c <= 1568 else (c // 2 if c // 2 <= 1568 else c // 4)
                src = w32[n][l].rearrange("r (a k) -> (r a) k", k=k)
                dst = wbf[n][l].rearrange("r (a k) -> (r a) k", k=k)
                DMA("pool", dst, src, [], [("wbf", n, l)], f"cast_{n}_{l}")
        MS("pool", ones_s, 1.0, [], ["ones_s"])
        MS("pool", ones_d, 1.0 / 1024, [], ["ones_d"])
        MS("pool", ones_c, 1.0 / 512, [], ["ones_c"])
        MS("pool", ones_r, 1.0 / 256, [], ["ones_r"])
        CP("dve", identb, ident_f, ["ident"], ["identb"])
        for l in range(depth):
            o = l * NCOL
            TT("dve", pcols[:, o + 202:o + 206], pcols[:, o + 0:o + 4], pcols[:, o + 14:o + 18], ALU.mult,
               ["pcols"], ["pcols"])

        def pc(l, j):
            return pcols[:, l * NCOL + j:l * NCOL + j + 1]

        uid = [0]
        outtoks = []

        def newsweep():
            P.barrier()
            A.reset()
            uid[0] += 1
            u = uid[0]
            return lambda *a: (u,) + a

        def layer_norm(T, zsrc, ztoks, C, ones_m, eps, gcol, bcol, dst_fn, dtoks, tmp, psA, psB, silu=False, phase=0,
                       pool_chain=False):
            zb, zsq, t1, t2, t3 = tmp["zb"], tmp["zsq"], tmp["t1"], tmp["t2"], tmp["t3"]
            if phase in (0, 1):
                for c in range(C):
                    CP("dve", zb[:, c, :], zsrc(c), ztoks(c), [T("zb", c)])
                    ACT(zsq[:, c, :], zsrc(c), AF.Square, ztoks(c), [T("zsq", c)])
            if phase == 1:
                return
            for c in range(C):
                MM(ps[psA][:, :], ones_m, zb[:, c, :], c == 0, c == C - 1, [T("zb", c), "ones"], PST[psA])
            for c in range(C):
                MM(ps[psB][:, :], ones_m, zsq[:, c, :], c == 0, c == C - 1, [T("zsq", c), "ones"], PST[psB])
            ACT(t1, ps[psA][:, :], AF.Square, [PST[psA]], [T("t1")])
            TT("dve", t2, ps[psB][:, :], t1, ALU.subtract, [PST[psB], T("t1")], [T("t2")])
            ACT(t2, t2, AF.Ln, [T("t2"), T("eps")], [T("t2")], bias=tmp["eps"])
            ACT(t1, t2, AF.Exp, [T("t2"), T("t1")], [T("t1")], scale=-0.5)
            STT(t3, ps[psA][:, :], -1.0, t1, ALU.mult, ALU.mult, [PST[psA], T("t1")], [T("t3")])
            for c in range(C):
                z = zsrc(c)
                zt = tmp["zt"][:, c % 2, :]
                if pool_chain:
                    TT("pool", zt, z, t1, ALU.mult, ztoks(c) + [T("t1")], [T("zt", c % 2)])
                    TT("pool", zt, zt, t3, ALU.add, [T("zt", c % 2), T("t3")], [T("zt", c % 2)])
                    TS("pool", dst_fn(c), zt, gcol(c), bcol(c), ALU.mult, ALU.add, [T("zt", c % 2), "pcols"], dtoks(c))
                    continue
                TT("pool", zt, z, t1, ALU.mult, ztoks(c) + [T("t1")], [T("zt", c % 2)])
                TT("dve", zt, zt, t3, ALU.add, [T("zt", c % 2), T("t3")], [T("zt", c % 2)])
                ACT(dst_fn(c), zt, AF.Silu if silu else AF.Identity, [T("zt", c % 2), "pcols"], dtoks(c),
                    scale=gcol(c), bias=bcol(c))

        def ln_tmp(C):
            d = dict(zb=A.bf(C * 512).rearrange("p (c t) -> p c t", c=C),
                     zsq=A.bf(C * 512).rearrange("p (c t) -> p c t", c=C),
                     t1=A.f32(512), t2=A.f32(512), t3=A.f32(512),
                     zt=A.f32(1024).rearrange("p (c t) -> p c t", c=2), eps=A.f32(1))
            return d

        for s in range(nseq):
            T = newsweep()
            xts = [A.f32(1024), A.f32(1024)]
            n_t = 0
            for tt in range(SEQ // 128 + MEM // 128):
                is_mem = tt >= SEQ // 128
                xt = xts[tt % 2]
                src = mem_d[s, (tt - 16) * 128:(tt - 15) * 128, :] if is_mem else x_d[s, tt * 128:(tt + 1) * 128, :]
                DMA("sp", xt, src, [], [T("xt", tt % 2)], f"xt{tt % 2}")
                for half in range(2):
                    bk = n_t % 8
                    n_t += 1
                    for j in range(4):
                        c = half * 4 + j
                        TR(ps[bk][:, j * 128:(j + 1) * 128], xt[:, c * 128:(c + 1) * 128], ident_f,
                           [T("xt", tt % 2), "ident"], PST[bk])
                    src_v = ps[bk][:, :].rearrange("p (j t) -> p j t", j=4)
                    if is_mem:
                        mt = tt - 16
                        CP("act", memT[:, half * 4:half * 4 + 4, mt * 128:(mt + 1) * 128], src_v, [PST[bk]], ["memT"])
                    else:
                        CP("dve", resid[:, half * 4:half * 4 + 4, tt * 128:(tt + 1) * 128], src_v, [PST[bk]],
                           [RT(tt // 4, half * 4 + j) for j in range(4)])

            for l in range(depth):
                T = newsweep()
                XBENG = ("dve", "dve")
                w1 = A.bf(8 * 544).rearrange("p (k n) -> p k n", k=8)
                wks = A.bf(8 * 96).rearrange("p (k n) -> p k n", k=8)
                wqa = A.bf(2 * 768).rearrange("p (k n) -> p k n", k=2)
                wqb = A.bf(2 * 768).rearrange("p (k n) -> p k n", k=2)
                wkv = A.bf(2 * 1024).rearrange("p (k n) -> p k n", k=2)
                KT = A.bf(8 * SEQ).rearrange("p (h t) -> p h t", h=8)
                VV = A.bf(16 * 512).rearrange("p (k n) -> p k n", k=16)
                yatt = A.bf(4 * SEQ).rearrange("p (c t) -> p c t", c=4)
                xb = A.bf(8 * 512).rearrange("p (c t) -> p c t", c=8)
                cg = A.bf(4 * 512).rearrange("p (c t) -> p c t", c=4)
                csq = A.bf(4 * 512).rearrange("p (c t) -> p c t", c=4)
                rq = A.f32(512)
                srt = A.f32(512)
                rkv = A.f32(512)
                cosT = A.f32(512)
                sinT = A.f32(512)
                posi = A.i32(512)
                ang = A.f32(512)
                kf = A.f32(512)
                qT = A.bf(8 * 512).rearrange("p (h t) -> p h t", h=8)
                tA, tB = ang, kf
                krb = posi.bitcast(BF16)[:, 0:512]
                exps = [csq[:, i, :] for i in range(3)]
                rcp = [A.f32(512) for _ in range(2)]
                rtok = A.f32(8)
                epsr = A.f32(1)
                MS("pool", epsr, RMS_EPS, [], [T("epsr")])
                win = wbf["mix_w_in"][l].rearrange("(k p) n -> p k n", p=128)
                wt = ("wbf", "mix_w_in", l)
                DMA("sp", w1, win[:, :, 0:544], [wt], [T("w1")], "wA")
                DMA("sp", wks[:, :, 0:64], win[:, :, 448:512], [wt], [T("wks")], "wB")
                DMA("sp", wks[:, :, 64:80], win[:, :, 528:544], [wt], [T("wks")], "wB", nonc=True)
                DMA("sp", wks[:, :, 80:96], win[:, :, 512:528], [wt], [T("wks")], "wB", nonc=True)
                wuq = wbf["mla_w_uq"][l].rearrange("(k p) n -> p k n", p=128)
                wt = ("wbf", "mla_w_uq", l)
                DMA("sp", wqa, wuq, [wt], [T("wqa")], "wC")
                DMA("sp", wqb, wuq, [wt], [T("wqb")], "wD")
                wqb4 = wqb.rearrange("p k (h d) -> p k h d", h=8)
                wuq4 = wuq.rearrange("p k (h d) -> p k h d", h=8)
                for k in range(2):
                    DMA("sp", wqb4[:, k, :, 64:80], wuq4[:, k, :, 80:96], [wt], [T("wqb")], "wD", nonc=True)
                    DMA("sp", wqb4[:, k, :, 80:96], wuq4[:, k, :, 64:80], [wt], [T("wqb")], "wD", nonc=True)
                DMA("sp", wkv, wbf["mla_w_ukv"][l].rearrange("(k p) n -> p k n", p=128), [("wbf", "mla_w_ukv", l)],
                    [T("wkv")], "wE")
                wkv4 = wkv.rearrange("p k (h d) -> p k h d", h=8)
                sc = 96.0 ** -0.5
                for b in range(NBLK):
                    t0 = b * BT
                    for c in range(8):
                        CP(XBENG[c % 2], xb[:, c, :], resid[:, c, t0:t0 + BT], [RT(b, c)], [T("xb", c)])
                    xbt = [T("xb", c) for c in range(8)]
                    R = slice(64, 96)
                    DMA("sp", posi[R, :], pos_d[s:s + 1, t0:t0 + BT].partition_broadcast(32), [], [T("posi")], "pos")
                    CP("dve", ang[R, :], posi[R, :], [T("posi")], [T("ang")])
                    TS("dve", ang[R, :], ang[R, :], cvec[R, 0:1], None, ALU.mult, None, [T("ang"), "cvec"], [T("ang")])
                    TS("dve", posi[R, :], ang[R, :], 1.0 / TWO_PI, None, ALU.mult, None, [T("ang")], [T("posi")])
                    CP("dve", kf[R, :], posi[R, :], [T("posi")], [T("kf")])
                    STT(ang[R, :], kf[R, :], -CW1, ang[R, :], ALU.mult, ALU.add, [T("kf"), T("ang")], [T("ang")])
                    STT(ang[R, :], kf[R, :], -CW2, ang[R, :], ALU.mult, ALU.add, [T("kf"), T("ang")], [T("ang")])
                    TS("dve", kf[R, :], ang[R, :], math.pi / 2, None, ALU.add, None, [T("ang")], [T("kf")])
                    TS("dve", cosT[R, :], kf[R, :], math.pi, -TWO_PI, ALU.is_gt, ALU.mult, [T("kf")], [T("cosT")])
                    TT("dve", kf[R, :], kf[R, :], cosT[R, :], ALU.add, [T("kf"), T("cosT")], [T("kf")])
                    TS("dve", kf[R, :], kf[R, :], PI_SAFE, -PI_SAFE, ALU.min, ALU.max, [T("kf")], [T("kf")])
                    TS("dve", ang[R, :], ang[R, :], PI_SAFE, -PI_SAFE, ALU.min, ALU.max, [T("ang")], [T("ang")])
                    ACT(cosT[R, :], kf[R, :], AF.Sin, [T("kf")], [T("cosT")])
                    ACT(sinT[R, :], ang[R, :], AF.Sin, [T("ang")], [T("sinT")])
                    TS("dve", sinT[R, :], sinT[R, :], cvec[R, 1:2], None, ALU.mult, None, [T("sinT"), "cvec"], [T("sinT")])
                    for j in range(4):
                        bk = j
                        for k in range(8):
                            MM(ps[bk][:, :], w1[:, k, j * 128:(j + 1) * 128], xb[:, k, :], k == 0, k == 7,
                               [T("w1"), xbt[k]], PST[bk])
                        ACT(csq[:, j, :], ps[bk][:, :], AF.Square, [PST[bk], "pcols"], [T("csq", j)], bias=pc(l, j))
                        ACT(cg[:, j, :], ps[bk][:, :], AF.Identity, [PST[bk], "pcols"], [T("cg", j)],
                            scale=pc(l, 14 + j), bias=pc(l, 202 + j))
                    for k in range(8):
                        MM(ps[4][0:96, :], w1[:, k, 448:544], xb[:, k, :], k == 0, k == 7, [T("w1"), xbt[k]], PST[4])
                    for k in range(8):
                        MM(ps[5][0:96, :], wks[:, k, :], xb[:, k, :], k == 0, k == 7, [T("wks"), xbt[k]], PST[5])
                    ACT(tA[R, :], ps[4][R, :], AF.Identity, [PST[4], "pcols"], [T("ang")], bias=pcols[R, l * NCOL + 4:l * NCOL + 5])
                    ACT(tB[R, :], ps[5][R, :], AF.Identity, [PST[5], "pcols"], [T("kf")], bias=pcols[R, l * NCOL + 5:l * NCOL + 6])
                    TT("dve", tA[R, :], tA[R, :], cosT[R, :], ALU.mult, [T("ang"), T("cosT")], [T("ang")])
                    TT("dve", tB[R, :], tB[R, :], sinT[R, :], ALU.mult, [T("kf"), T("sinT")], [T("kf")])
                    TT("dve", krb[R, :], tA[R, :], tB[R, :], ALU.add, [T("ang"), T("kf")], [T("posi")])
                    for h in range(NH):
                        CP("dve" if h % 2 == 0 else "act", KT[R, h, t0:t0 + BT], krb[R, :], [T("posi")], [T("KT", h, b)])
                    for (j0, bk, dst) in ((0, 6, rq), (2, 7, rkv)):
                        for j in range(2):
                            MM(ps[bk][:, :], ones_r, csq[:, j0 + j, :], j == 0, j == 1, [T("csq", j0 + j), "ones_r"], PST[bk])
                        ACT(dst, ps[bk][:, :], AF.Ln, [PST[bk], T("epsr")], [T("r", j0)], bias=epsr)
                        ACT(dst, dst, AF.Exp, [T("r", j0)], [T("r", j0)], scale=-0.5)
                    for tt in range(4):
                        for j in range(2):
                            MM(ps[6][:, tt:tt + 1], csq[:, 2 + j, tt * 128:(tt + 1) * 128], ones_r[:, 0:1], j == 0, j == 1,
                               [T("csq", 2 + j), "ones_r", T("r", 0)], PST[6])
                    ACT(rtok[:, 0:4], ps[6][:, 0:4], AF.Ln, [PST[6], T("epsr")], [T("rtok")], bias=epsr)
                    ACT(rtok[:, 0:4], rtok[:, 0:4], AF.Exp, [T("rtok")], [T("rtok")], scale=-0.5)
                    TT("dve", srt[R, :], rq[R, :], sinT[R, :], ALU.mult, [T("r", 0), T("sinT")], [T("srt")])
                    TT("dve", rq[R, :], rq[R, :], cosT[R, :], ALU.mult, [T("r", 0), T("cosT"), T("srt")], [T("r", 0)])
                    for h in range(NH):
                        ba, bb = 2 * (h % 2), 2 * (h % 2) + 1
                        for k in range(2):
                            MM(ps[ba][0:96, :], wqa[:, k, h * 96:(h + 1) * 96], cg[:, k, :], k == 0, k == 1,
                               [T("wqa"), T("cg", k)], PST[ba])
                        for k in range(2):
                            MM(ps[bb][0:96, :], wqb[:, k, h * 96:(h + 1) * 96], cg[:, k, :], k == 0, k == 1,
                               [T("wqb"), T("cg", k)], PST[bb])
                        tq, tqt = (tA, T("ang")) if h % 2 == 0 else (tB, T("kf"))
                        TT("dve", qT[0:96, h, :], ps[ba][0:96, :], rq[0:96, :], ALU.mult, [PST[ba], T("r", 0)], [T("q", h)])
                        TT("dve", tq[R, :], ps[bb][R, :], srt[R, :], ALU.mult, [PST[bb], T("srt")], [tqt])
                        TT("dve", qT[R, h, :], qT[R, h, :], tq[R, :], ALU.add, [T("q", h), tqt], [T("q", h)])
                    for h in range(NH):
                        bk = 4 + (h % 2)
                        for k in range(2):
                            MM(ps[bk][0:64, :], wkv4[:, k, h, 0:64], cg[:, 2 + k, :], k == 0, k == 1,
                               [T("wkv"), T("cg", 2 + k)], PST[bk])
                        TT("dve", KT[0:64, h, t0:t0 + BT], ps[bk][0:64, :], rkv[0:64, :], ALU.mult, [PST[bk], T("r", 2)],
                           [T("KT", h, b)])
                    for tt in range(4):
                        bk = 6 + (tt % 2)
                        for k in range(2):
                            MM(ps[bk][:, :].rearrange("p (h d) -> p h d", h=8), cg[:, 2 + k, tt * 128:(tt + 1) * 128],
                               wkv4[:, k, :, 64:128], k == 0, k == 1, [T("wkv"), T("cg", 2 + k), T("rtok")], PST[bk])
                        ACT(VV[:, 4 * b + tt, :], ps[bk][:, :], AF.Identity, [PST[bk], T("rtok")], [T("V", 4 * b + tt)],
                            scale=rtok[:, tt:tt + 1])
                    pairs = []
                    for h in range(NH):
                        for kt in range(4 * b + 4):
                            pairs.append((h, kt))
                    nkt = 4 * b + 4

                    def score(i):
                        h, kt = pairs[i]
                        c0 = max(0, kt - 4 * b) * 128
                        bk = i % 3
                        MM(ps[bk][:, c0:BT], KT[0:96, h, kt * 128:(kt + 1) * 128], qT[0:96, h, c0:BT], True, True,
                           [T("KT", h, kt // 4), T("q", h)], PST[bk])
                        e = exps[i % 3]
                        ACT(e[:, c0:BT], ps[bk][:, c0:BT], AF.Exp, [PST[bk]], [T("csq", i % 3)], scale=sc)
                        if kt >= 4 * b:
                            MS("pool", e[64:128, c0:c0 + 64], 0.0, [], [T("csq", i % 3)])

                    def pv(i):
                        h, kt = pairs[i]
                        c0 = max(0, kt - 4 * b) * 128
                        e = exps[i % 3]
                        bo, bs = 3 + (h % 2), 5 + (h % 2)
                        if h % 2 == 0:
                            MM(ps[bo][0:64, c0:BT], VV[:, kt, h * 64:(h + 1) * 64], e[:, c0:BT], kt == 0, kt == nkt - 1,
                               [T("V", kt), T("csq", i % 3)], PST[bo])
                        else:
                            MM(ps[bo][:, c0:BT], VV[:, kt, (h - 1) * 64:(h + 1) * 64], e[:, c0:BT], kt == 0, kt == nkt - 1,
                               [T("V", kt), T("csq", i % 3)], PST[bo])
                        MM(ps[bs][:, c0:BT], ones_s, e[:, c0:BT], kt == 0, kt == nkt - 1, ["ones_s", T("csq", i % 3)], PST[bs])
                        if kt == nkt - 1:
                            rr = slice(0, 64) if h % 2 == 0 else slice(64, 128)
                            rc = rcp[h % 2]
                            ACT(rc[rr, :], ps[bs][rr, :], AF.Ln, [PST[bs]], [T("rc", h % 2)])
                            ACT(rc[rr, :], rc[rr, :], AF.Exp, [T("rc", h % 2)], [T("rc", h % 2)], scale=-1.0)
                            TT("dve", yatt[rr, h // 2, t0:t0 + BT], ps[bo][rr, :], rc[rr, :], ALU.mult,
                               [PST[bo], T("rc", h % 2)], [("yatt", b, h // 2)])

                    LA = 2
                    for i in range(len(pairs) + LA):
                        if i < len(pairs):
                            score(i)
                        if i >= LA:
                            pv(i - LA)
                if stop_after == (l, 1):
                    break

                T = newsweep()
                XBENG = XB23
                A.off = A.base
                _skip = A._take(8 * 544 // 2 + 8 * 96 // 2 + 768 + 768 + 1024 + 8 * SEQ // 2 + 16 * 512 // 2)
                yatt = A.bf(4 * SEQ).rearrange("p (c t) -> p c t", c=4)
                yatt_end = A.off
                A.off = A.base
                wu = A.bf(8 * 1024).rearrange("p (k n) -> p k n", k=8)
                wo = A.bf(8 * 1024).rearrange("p (k n) -> p k n", k=8)
                dg = [A.bf(31 * 128).rearrange("p (t n) -> p t n", t=31) for _ in range(4)]
                sg = [A.f32(512)] * 2
                assert A.off <= yatt_end - 4 * SEQ // 2, "sweep2 prefix overlaps yatt"
                A.off = yatt_end
                xb = A.bf(8 * 512).rearrange("p (c t) -> p c t", c=8)
                hh = A.bf(4 * 544).rearrange("p (c t) -> p c t", c=4)
                cv = A.f32(4 * 512).rearrange("p (c t) -> p c t", c=4)
                ycv = A.bf(4 * 512).rearrange("p (c t) -> p c t", c=4)
                lt = ln_tmp(8)
                MS("pool", lt["eps"], LN_EPS, [], [T("eps")])
                win = wbf["mix_w_in"][l].rearrange("(k p) n -> p k n", p=128)
                DMA("sp", wu, win[:, :, 544:1568], [("wbf", "mix_w_in", l)], [T("wu")], "wA")
                DMA("sp", wo, wbf["mix_w_o"][l].rearrange("(k p) n -> p k n", p=128), [("wbf", "mix_w_o", l)], [T("wo")], "wB")
                MS("pool", hh[:, :, 0:32], 0.0, [], [T("hh", c) for c in range(4)])
                deferred = []
                xbt = [T("xb", c) for c in range(8)]

                def stA1(b):
                    t0 = b * BT
                    for c in range(8):
                        CP(XBENG[c % 2], xb[:, c, :], resid[:, c, t0:t0 + BT], [RT(b, c)], [T("xb", c)])

                def stA2(b):
                    if b == 0:
                        for c in range(4):
                            for tp in range(31):
                                TS("pool" if c < 2 else "act", dg[c][:, tp, :], identb, pc(l, 78 + c * 31 + tp), None, ALU.mult, None,
                                   ["identb", "pcols"], [T("dg", c)])
                    if b > 0:
                        for c in range(4):
                            CP("pool", hh[:, c, 2:32], hh[:, c, 514:544], [T("hh", c)], [T("hh", c)])
                    for c in range(4):
                        bg_, ba_ = (c % 3) * 2, (c % 3) * 2 + 1
                        for k in range(8):
                            MM(ps[bg_][:, :], wu[:, k, 512 + c * 128:512 + (c + 1) * 128], xb[:, k, :], k == 0, k == 7,
                               [T("wu"), xbt[k]], PST[bg_])
                        for k in range(8):
                            MM(ps[ba_][:, :], wu[:, k, c * 128:(c + 1) * 128], xb[:, k, :], k == 0, k == 7,
                               [T("wu"), xbt[k]], PST[ba_])
                        ACT(sg[c % 2], ps[bg_][:, :], AF.Sigmoid, [PST[bg_], "pcols"], [T("sg")], bias=pc(l, 10 + c))
                        STT(hh[:, c, 32:544], ps[ba_][:, :], pc(l, 6 + c), sg[c % 2], ALU.add, ALU.mult,
                            [PST[ba_], T("sg"), "pcols"], [T("hh", c)])

                def stB(b):
                    for c in range(4):
                        d = dg[c]
                        bk = (4, 5, 0, 1)[c]
                        for tp in range(31):
                            MM(ps[bk][:, :], d[:, tp, :], hh[:, c, 2 + tp:2 + tp + BT], tp == 0, tp == 30,
                               [T("dg", c), T("hh", c)], PST[bk])
                        ACT(cv[:, c, :], ps[bk][:, :], AF.Identity, [PST[bk], "pcols"], [T("cv", c)], bias=pc(l, 18 + c))

                def stC(b, phase):
                    layer_norm(T, lambda c: cv[:, c, :], lambda c: [T("cv", c)], 4, ones_c, LN_EPS,
                               lambda c: pc(l, 22 + c), lambda c: pc(l, 26 + c), lambda c: ycv[:, c, :],
                               lambda c: [T("ycv", c)], lt, 6, 7, silu=True, phase=phase)

                def stD(b):
                    t0 = b * BT
                    for o in range(8):
                        bk = o % 4
                        for k in range(8):
                            rhs = yatt[:, k, t0:t0 + BT] if k < 4 else ycv[:, k - 4, :]
                            rt = ("yatt", b, k) if k < 4 else T("ycv", k - 4)
                            MM(ps[bk][:, :], wo[:, k, o * 128:(o + 1) * 128], rhs, k == 0, k == 7, [T("wo"), rt], PST[bk])
                        STT(resid[:, o, t0:t0 + BT], resid[:, o, t0:t0 + BT], ALPHA, ps[bk][:, :], ALU.mult, ALU.add,
                            [PST[bk], RT(b, o)], [RT(b, o)])

                def stL(b, phase):
                    t0 = b * BT
                    layer_norm(T, lambda c: resid[:, c, t0:t0 + BT], lambda c: [RT(b, c)], 8, ones_d, LN_EPS,
                               lambda c: pc(l, 30 + c), lambda c: pc(l, 38 + c), lambda c: resid[:, c, t0:t0 + BT],
                               lambda c: [RT(b, c)], lt, 6, 7, phase=phase, pool_chain=(b < NBLK - 1))

                stA1(0)
                stA2(0)
                stA1(1)
                stB(0)
                stC(0, 1)
                for b in range(NBLK):
                    nxt = b + 1 < NBLK
                    stC(b, 2)
                    if nxt:
                        stA2(b + 1)
                    stD(b)
                    stL(b, 1)
                    if b + 2 < NBLK:
                        stA1(b + 2)
                    stL(b, 2)
                    if nxt:
                        stB(b + 1)
                        stC(b + 1, 1)
                if stop_after == (l, 2):
                    break

                T = newsweep()
                wq = A.bf(8 * 1024).rearrange("p (k n) -> p k n", k=8)
                wo = A.bf(8 * 1024).rearrange("p (k n) -> p k n", k=8)
                kx = A.bf(8 * MEM).rearrange("p (c t) -> p c t", c=8)
                vx = A.bf(2 * 1024).rearrange("p (m n) -> p m n", m=2)
                xb = A.bf(8 * 512).rearrange("p (c t) -> p c t", c=8)
                qx = A.bf(8 * 512).rearrange("p (c t) -> p c t", c=8)
                yx = A.bf(8 * 512).rearrange("p (c t) -> p c t", c=8)
                exps = [A.bf(512) for _ in range(4)]
                rcp = [A.f32(512) for _ in range(2)]
                lt = ln_tmp(8)
                wkvx2d = A.bf(8 * 2048)
                wkvx = wkvx2d.rearrange("p (k n) -> p k n", k=8)
                qxs = [qx, wkvx2d[:, 0:4096].rearrange("p (c t) -> p c t", c=8)]
                yxs = [yx, wkvx2d[:, 4096:8192].rearrange("p (c t) -> p c t", c=8)]
                MS("pool", lt["eps"], LN_EPS, [], [T("eps")])
                if s == 0 and RL_ENG != "none":
                    srci = wbf["ffn_w_in"][l].rearrange("(k p) n -> p k n", p=128)
                    for g in range(NHC // 2):
                        dsti = wrl["ffn_w_in"][l, g].rearrange("p (k n) -> p k n", k=8)
                        DMA(RL_ENG, dsti[:, :, 0:256], srci[:, :, g * 256:(g + 1) * 256], [("wbf", "ffn_w_in", l)],
                            [("wrl", "ffn_w_in", l)], f"rl_{l}", grouped=True)
                        DMA(RL_ENG, dsti[:, :, 256:512], srci[:, :, FFN_H + g * 256:FFN_H + (g + 1) * 256], [("wbf", "ffn_w_in", l)],
                            [("wrl", "ffn_w_in", l)], f"rl_{l}", grouped=True)
                    srcd = wbf["ffn_w_down"][l].rearrange("(j p) (o n) -> o p j n", p=128, n=128)
                    for o in range(8):
                        dstd = wrl["ffn_w_down"][l, o].rearrange("p (j n) -> p j n", j=NHC)
                        DMA(RL_ENG, dstd, srcd[o], [("wbf", "ffn_w_down", l)], [("wrl", "ffn_w_down", l)], f"rl_{l}", grouped=True, nonc=False)
                wkv_src = wbf["xa_w_kv"][l].rearrange("(k p) n -> p k n", p=128)
                DMA("sp", wkvx[:, :, 0:1024], wkv_src[:, :, 0:1024], [("wbf", "xa_w_kv", l)], [T("wkvxK")], "wA")
                DMA("sp", wkvx[:, :, 1024:2048], wkv_src[:, :, 1024:2048], [("wbf", "xa_w_kv", l)], [T("wkvxV")], "wD")
                DMA("sp", wq, wbf["xa_w_q"][l].rearrange("(k p) n -> p k n", p=128), [("wbf", "xa_w_q", l)], [T("wq")], "wB")
                DMA("sp", wo, wbf["xa_w_o"][l].rearrange("(k p) n -> p k n", p=128), [("wbf", "xa_w_o", l)], [T("wo")], "wC")
                for oc in range(8):
                    bk = oc % 4
                    for k in range(8):
                        MM(ps[bk][:, 0:MEM], wkvx[:, k, oc * 128:(oc + 1) * 128], memT[:, k, :], k == 0, k == 7,
                           [T("wkvxK"), "memT"], PST[bk])
                    CP("act", kx[:, oc, :], ps[bk][:, 0:MEM], [PST[bk]], [T("kx")])
                for mt in range(2):
                    for hv in range(2):
                        bk = 4 + (2 * mt + hv)
                        for k in range(8):
                            MM(ps[bk][:, :], memT[:, k, mt * 128:(mt + 1) * 128], wkvx[:, k, 1024 + hv * 512:1024 + (hv + 1) * 512],
                               k == 0, k == 7, [T("wkvxV"), "memT"], PST[bk])
                        CP("dve", vx[:, mt, hv * 512:(hv + 1) * 512], ps[bk][:, :], [PST[bk]], [T("vx")])
                xbt = [T("xb", c) for c in range(8)]

                def xA1(b):
                    t0 = b * BT
                    for c in range(8):
                        CP(XBENG[c % 2], xb[:, c, :], resid[:, c, t0:t0 + BT], [RT(b, c)], [T("xb", c)])

                def xA2(b):
                    q_ = qxs[b % 2]
                    for o in range(8):
                        bk = 6 + o % 2
                        for k in range(8):
                            MM(ps[bk][:, :], wq[:, k, o * 128:(o + 1) * 128], xb[:, k, :], k == 0, k == 7, [T("wq"), xbt[k]], PST[bk])
                        CP("act" if o % 2 == 0 else "dve", q_[:, o, :], ps[bk][:, :], [PST[bk]], [T("qx", b % 2, o), T("wkvxK"), T("wkvxV")])

                def xs(b, h):
                    q_ = qxs[b % 2]
                    for mt in range(2):
                        bk = (h % 2) * 2 + mt
                        for dc in range(2):
                            MM(ps[bk][:, :], kx[:, 2 * h + dc, mt * 128:(mt + 1) * 128], q_[:, 2 * h + dc, :], dc == 0, dc == 1,
                               [T("kx"), T("qx", b % 2, 2 * h + dc)], PST[bk])
                        ei = (2 * h + mt) % 4
                        ACT(exps[ei], ps[bk][:, :], AF.Exp, [PST[bk]], [T("e", ei)], scale=1.0 / 16.0)

                def xpv(b, h):
                    y_ = yxs[b % 2]
                    for dvc in range(2):
                        bk = 4 + dvc
                        for mt in range(2):
                            ei = (2 * h + mt) % 4
                            MM(ps[bk][:, :], vx[:, mt, (2 * h + dvc) * 128:(2 * h + dvc + 1) * 128], exps[ei], mt == 0, mt == 1,
                               [T("vx"), T("e", ei)], PST[bk])
                    bs = 6 + (h % 2)
                    for mt in range(2):
                        ei = (2 * h + mt) % 4
                        MM(ps[bs][:, :], ones_s, exps[ei], mt == 0, mt == 1, ["ones_s", T("e", ei)], PST[bs])
                    rc = rcp[h % 2]
                    ACT(rc, ps[bs][:, :], AF.Ln, [PST[bs]], [T("rc", h % 2)])
                    ACT(rc, rc, AF.Exp, [T("rc", h % 2)], [T("rc", h % 2)], scale=-1.0)
                    for dvc in range(2):
                        TT("dve", y_[:, 2 * h + dvc, :], ps[4 + dvc][:, :], rc, ALU.mult, [PST[4 + dvc], T("rc", h % 2)],
                           [T("yx", b % 2, 2 * h + dvc), T("wkvxK"), T("wkvxV")])

                def xD(b):
                    t0 = b * BT
                    y_ = yxs[b % 2]
                    for o in range(8):
                        bk = o % 2
                        for k in range(8):
                            MM(ps[bk][:, :], wo[:, k, o * 128:(o + 1) * 128], y_[:, k, :], k == 0, k == 7,
                               [T("wo"), T("yx", b % 2, k)], PST[bk])
                        STT(resid[:, o, t0:t0 + BT], resid[:, o, t0:t0 + BT], ALPHA, ps[bk][:, :], ALU.mult, ALU.add,
                            [PST[bk], RT(b, o)], [RT(b, o)])

                def xL(b, phase):
                    t0 = b * BT
                    layer_norm(T, lambda c: resid[:, c, t0:t0 + BT], lambda c: [RT(b, c)], 8, ones_d, LN_EPS,
                               lambda c: pc(l, 46 + c), lambda c: pc(l, 54 + c), lambda c: resid[:, c, t0:t0 + BT],
                               lambda c: [RT(b, c)], lt, 2, 3, phase=phase, pool_chain=(b < NBLK - 1))

                xA1(0)
                xA2(0)
                xA1(1)
                for b in range(NBLK):
                    xs(b, 0)
                    xs(b, 1)
                    if b + 1 < NBLK:
                        xA2(b + 1)
                    if b > 0:
                        xL(b - 1, 2)
                    xpv(b, 0)
                    xs(b, 2)
                    xpv(b, 1)
                    xs(b, 3)
                    xpv(b, 2)
                    xpv(b, 3)
                    xD(b)
                    xL(b, 1)
                    if b + 2 < NBLK:
                        xA1(b + 2)
                xL(NBLK - 1, 2)
                if stop_after == (l, 3):
                    break

                T = newsweep()
                xb2 = A.bf(8 * 1024).rearrange("p (c t) -> p c t", c=8)
                hT = A.bf(NHC * 1024).rearrange("p (c t) -> p c t", c=NHC)
                wring2 = [A.bf(8 * 512) for _ in range(3)]
                dring2 = [A.bf(NHC * 128) for _ in range(2)]
                wring = [w.rearrange("p (k n) -> p k n", k=8) for w in wring2]
                dring = [w.rearrange("p (k n) -> p k n", k=NHC) for w in dring2]
                sl = [A.f32(512) for _ in range(3)]
                lt = ln_tmp(8)
                ots = [A.f32(1024), A.f32(1024)]
                MS("pool", lt["eps"], LN_EPS, [], [T("eps")])
                wfi = wbf["ffn_w_in"][l].rearrange("(k p) n -> p k n", p=128)
                wfd = wbf["ffn_w_down"][l].rearrange("(k p) n -> p k n", p=128)
                last = (l == depth - 1)
                n_sl = 0
                def xb2_cast(hf):
                    for c in range(8):
                        CP("act" if c % 2 == 0 else "dve", xb2[:, c, :], resid[:, c, hf * 1024:(hf + 1) * 1024],
                           [RT(2 * hf, c), RT(2 * hf + 1, c)], [T("xb2", c)])

                def ln3(phase, b):
                    t0 = b * BT
                    layer_norm(T, lambda c: resid[:, c, t0:t0 + BT], lambda c: [RT(b, c)], 8, ones_d, LN_EPS,
                               lambda c: pc(l, 62 + c), lambda c: pc(l, 70 + c), lambda c: resid[:, c, t0:t0 + BT],
                               lambda c: [RT(b, c)], lt, 6, 7, phase=phase, pool_chain=(b < 2))

                def emit_out(b, banks=(4, 5)):
                    t0 = b * BT
                    for tt in range(4):
                        oi = n_ot[0] % 2
                        ot = ots[oi]
                        n_ot[0] += 1
                        for half in range(2):
                            bk = banks[(2 * tt + half) % len(banks)]
                            for j in range(4):
                                c = half * 4 + j
                                TR(ps[bk][:, j * 128:(j + 1) * 128], resid[:, c, t0 + tt * 128:t0 + (tt + 1) * 128], ident_f,
                                   [RT(b, c), "ident"], PST[bk])
                            CP("dve" if half == 0 else "act", ot[:, half * 512:(half + 1) * 512], ps[bk][:, :], [PST[bk]],
                               [T("ot", oi, half)])
                        outtoks.append(("outdram", len(outtoks)))
                        DMA("sp", out_d[s, t0 + tt * 128:t0 + (tt + 1) * 128, :], ot, [T("ot", oi, 0), T("ot", oi, 1)],
                            [outtoks[-1]], f"ot{oi}")

                def finish(b):
                    ln3(2, b)
                    if last:
                        emit_out(b, banks=(0, 1, 2, 3, 4, 5))

                n_ot = [0]
                xbt = [T("xb2", c) for c in range(8)]
                pending = []
                xb2_cast(0)
                for hf in range(2):
                    for g in range(NHC // 2):
                        ri = (hf * (NHC // 2) + g) % 3
                        wr = wring[ri]
                        DMA("sp", wr[:, :, 0:256], wfi[:, :, g * 256:(g + 1) * 256], [("wbf", "ffn_w_in", l)], [T("wr", ri)], f"wr{ri}a")
                        DMA("sp", wr[:, :, 256:512], wfi[:, :, FFN_H + g * 256:FFN_H + (g + 1) * 256], [("wbf", "ffn_w_in", l)],
                            [T("wr", ri)], f"wr{ri}b")
                        for bb in range(2):
                            for cc in range(2):
                                j = 2 * g + cc
                                bg_, bu_ = 2 * ((2 * bb + cc) % 2), 2 * ((2 * bb + cc) % 2) + 1
                                for k in range(8):
                                    MM(ps[bg_][:, :], wr[:, k, cc * 128:(cc + 1) * 128], xb2[:, k, bb * 512:(bb + 1) * 512],
                                       k == 0, k == 7, [T("wr", ri), xbt[k]], PST[bg_])
                                for k in range(8):
                                    MM(ps[bu_][:, :], wr[:, k, 256 + cc * 128:256 + (cc + 1) * 128], xb2[:, k, bb * 512:(bb + 1) * 512],
                                       k == 0, k == 7, [T("wr", ri), xbt[k]], PST[bu_])
                                si = n_sl % 3
                                n_sl += 1
                                ACT(sl[si], ps[bg_][:, :], AF.Silu, [PST[bg_]], [T("sl", si)])
                                TT("dve", hT[:, j, bb * 512:(bb + 1) * 512], ps[bu_][:, :], sl[si], ALU.mult,
                                   [PST[bu_], T("sl", si)], [T("hT", j, bb)])
                        if g in (1, 3, 5) and pending:
                            pending.pop(0)()
                    while pending:
                        pending.pop(0)()
                    if hf == 0:
                        xb2_cast(1)
                    for o in range(8):
                        di = (hf * 8 + o) % 2
                        dr = dring[di]
                        DMA("sp", dr, wfd[:, :, o * 128:(o + 1) * 128], [("wbf", "ffn_w_down", l)], [T("dr", di)], f"dr{di}")
                        for bb in range(2):
                            b = 2 * hf + bb
                            bk = 4 + (2 * o + bb) % 4
                            for j in range(NHC):
                                MM(ps[bk][:, :], dr[:, j, :], hT[:, j, bb * 512:(bb + 1) * 512], j == 0, j == NHC - 1,
                                   [T("dr", di), T("hT", j, bb)], PST[bk])
                            STT(resid[:, o, b * BT:(b + 1) * BT], resid[:, o, b * BT:(b + 1) * BT], ALPHA, ps[bk][:, :],
                                ALU.mult, ALU.add, [PST[bk], RT(b, o)], [RT(b, o)])
                    b0, b1 = 2 * hf, 2 * hf + 1
                    ln3(1, b0)
                    if hf == 0:
                        def item0(b0=b0, b1=b1):
                            ln3(2, b0)
                            ln3(1, b1)

                        def item1(b0=b0, b1=b1):
                            if last:
                                emit_out(b0)
                            ln3(2, b1)

                        def item2(b1=b1):
                            if last:
                                emit_out(b1)
                        pending.append(item0)
                        pending.append(item1)
                        pending.append(item2)
                    else:
                        ln3(2, b0)
                        ln3(1, b1)
                        ln3(2, b1)
                        if last:
                            emit_out(b0, banks=(0, 1, 2, 3, 4, 5))
                            emit_out(b1, banks=(0, 1, 2, 3, 4, 5))
                if stop_after == (l, 4):
                    break
            if stop_after is not None:
                break
        if stop_after is not None:
            P.barrier()
            if stop_after[1] == 1:
                for c in range(4):
                    CP("dve", resid[:, c, :], yatt[:, c, :], [("yatt", bb, c) for bb in range(4)], [RT(bb, c) for bb in range(4)])
            dv = out_d[0].rearrange("(a b) d -> a (b d)", a=1024)
            for c in range(8):
                outtoks.append(("outdram", len(outtoks)))
                DMA("sp", dv[c * 128:(c + 1) * 128, :], resid[:, c, :], [RT(bb, c) for bb in range(4)], [outtoks[-1]], "dbg")
        P.barrier()
        P.op("sp", lambda e: e.nop(), list(outtoks), [])
        P.emit(st)
        build.info = dict(n_ops=len(P.ops), n_sems=P.n_sems, peak=A.peak)
    return nc


def _host_layout(inp):
    L = DEPTH
    pcols = np.zeros((128, L * NCOL), np.float32)

    def put(l, j, vec):
        v = np.asarray(vec, np.float32).reshape(-1, 128)
        for i in range(v.shape[0]):
            pcols[:, l * NCOL + j + i] = v[i]
    for l in range(L):
        b_in = np.asarray(inp["mix_b_in"][l], np.float32)
        put(l, 0, b_in[0:512])
        pcols[64:96, l * NCOL + 4] = b_in[512:544]
        pcols[64:80, l * NCOL + 5] = b_in[528:544]
        pcols[80:96, l * NCOL + 5] = b_in[512:528]
        put(l, 6, b_in[544:1056])
        put(l, 10, b_in[1056:1568])
        put(l, 14, inp["mla_q_norm"][l])
        put(l, 16, inp["mla_kv_norm"][l])
        put(l, 18, inp["conv_dw_b"][l])
        put(l, 22, inp["conv_norm_g"][l])
        put(l, 26, inp["conv_norm_b"][l])
        put(l, 30, inp["ln1_g"][l]); put(l, 38, inp["ln1_b"][l])
        put(l, 46, inp["ln2_g"][l]); put(l, 54, inp["ln2_b"][l])
        put(l, 62, inp["ln3_g"][l]); put(l, 70, inp["ln3_b"][l])
        dw = np.asarray(inp["conv_dw_w"][l], np.float32)
        for c in range(4):
            pcols[:, l * NCOL + 78 + c * 31:l * NCOL + 78 + (c + 1) * 31] = dw[:, c * 128:(c + 1) * 128].T
    cvec = np.zeros((128, 4), np.float32)
    inv_freq = (np.float32(10000.0) ** (-np.arange(0, 32, 2, dtype=np.float32) / np.float32(32))).astype(np.float32)
    for p in range(128):
        cvec[p, 0] = inv_freq[p % 16]
        cvec[p, 1] = -1.0 if (p % 32) < 16 else 1.0
    return pcols, cvec


_NC_CACHE = {}


def kernel(**inputs):
    nseq = 32 // N_CORES
    if "nc" not in _NC_CACHE:
        _NC_CACHE["nc"] = build(nseq=nseq, depth=DEPTH)
    nc = _NC_CACHE["nc"]
    pcols, cvec = _host_layout(inputs)
    ident = np.eye(128, dtype=np.float32)
    x = np.ascontiguousarray(np.asarray(inputs["x"], np.float32))
    mem = np.ascontiguousarray(np.asarray(inputs["mem"], np.float32))
    pos = np.ascontiguousarray(np.asarray(inputs["positions"], np.int32))
    shared = {n: np.ascontiguousarray(np.asarray(inputs[n], np.float32)) for n in WNAMES}
    in_maps = []
    for i in range(N_CORES):
        m = dict(shared)
        m["x"] = x[i * nseq:(i + 1) * nseq]
        m["mem"] = mem[i * nseq:(i + 1) * nseq]
        m["positions"] = pos[i * nseq:(i + 1) * nseq]
        m["pcols"] = pcols
        m["cvec"] = cvec
        m["ident"] = ident
        in_maps.append(m)
    res = run_bass_kernel_spmd(nc, in_maps, core_ids=list(range(N_CORES)))
    return np.concatenate([np.asarray(r["out"], np.float32) for r in res.results], axis=0)
```
